# Optimizing a Trainium2 kernel written in Bass

```python
import math
import jax, jax.numpy as jnp
from jax import lax
import numpy as np

D_MODEL = 1024
BATCH = 8
SEQ = 2048
DEPTH = 1
DEC_BATCH = 16
DEC_SEQ = 2048
PAST_LEN = 128

HEAD_DIM = 64
DIFF_HEADS = 4
WIN_HEADS = 8
WIN_KV_HEADS = 2
WIN_GROUP = WIN_HEADS // WIN_KV_HEADS
WINDOW = 128
BLOCK = 128
N_BUCKETS = 32
MAX_DISTANCE = 128
N_BIAS_HEADS = DIFF_HEADS + WIN_HEADS
D_FF = 2816
CONV_WIDTH = 3
EPS = 1e-6
NEG_INF = -1e30

DIFF_QK_W = DIFF_HEADS * 2 * HEAD_DIM
DIFF_V_W = DIFF_HEADS * 2 * HEAD_DIM
WIN_Q_W = WIN_HEADS * HEAD_DIM
WIN_KV_W = WIN_KV_HEADS * HEAD_DIM
D_IN = 2 * DIFF_QK_W + DIFF_V_W + WIN_Q_W + 2 * WIN_KV_W
D_MIX = DIFF_V_W + WIN_Q_W
SPLIT_POINTS = (DIFF_QK_W, 2 * DIFF_QK_W, 2 * DIFF_QK_W + DIFF_V_W,
                2 * DIFF_QK_W + DIFF_V_W + WIN_Q_W,
                2 * DIFF_QK_W + DIFF_V_W + WIN_Q_W + WIN_KV_W)

kernel_name = 'hymba_diff_window_encoder'


def rms_norm(x, g):
    xf = x.astype(jnp.float32)
    y = xf * lax.rsqrt(jnp.mean(xf * xf, axis=-1, keepdims=True) + EPS)
    return (y * g.astype(jnp.float32)).astype(x.dtype)


def rel_bucket(rel):
    nb = N_BUCKETS // 2
    max_exact = nb // 2
    ret = jnp.where(rel > 0, nb, 0)
    n = jnp.abs(rel)
    nf = jnp.maximum(n, 1).astype(jnp.float32)
    large = max_exact + (jnp.log(nf / max_exact) / math.log(MAX_DISTANCE / max_exact)
                         * (nb - max_exact)).astype(jnp.int32)
    large = jnp.minimum(large, nb - 1)
    return ret + jnp.where(n < max_exact, n, large)


def diff_attention(q, k, v, bias_table, lam):
    B, S = q.shape[0], q.shape[1]
    nb = S // BLOCK
    scale = HEAD_DIM ** -0.5
    table = bias_table[:, :DIFF_HEADS]
    kpos = jnp.arange(S)
    qb = q.reshape(B, nb, BLOCK, DIFF_HEADS, 2, HEAD_DIM).transpose(1, 0, 2, 3, 4, 5)
    lam32 = lam.astype(jnp.float32)

    def one_block(args):
        n, qblk = args
        qpos = n * BLOCK + jnp.arange(BLOCK)
        bias = table[rel_bucket(kpos[None, :] - qpos[:, None])]
        s = jnp.einsum('bqhmd,bkhmd->bhmqk', qblk, k).astype(jnp.float32) * scale
        s = s + bias.transpose(2, 0, 1)[None, :, None].astype(jnp.float32)
        p = jax.nn.softmax(s, axis=-1)
        a = p[:, :, 0] - lam32 * p[:, :, 1]
        return jnp.einsum('bhqk,bkhe->bqhe', a.astype(v.dtype), v)

    out = lax.map(one_block, (jnp.arange(nb), qb))
    return out.transpose(1, 0, 2, 3, 4).reshape(B, S, DIFF_HEADS, 2 * HEAD_DIM)


def window_attention(q, k, v, bias_table, sink):
    B, S = q.shape[0], q.shape[1]
    nb = S // BLOCK
    scale = HEAD_DIM ** -0.5
    qb = q.reshape(B, nb, BLOCK, WIN_KV_HEADS, WIN_GROUP, HEAD_DIM)

    def band(t):
        tp = jnp.pad(t, ((0, 0), (BLOCK, BLOCK), (0, 0), (0, 0)))
        tp = tp.reshape(B, nb + 2, BLOCK, WIN_KV_HEADS, HEAD_DIM)
        return jnp.concatenate([tp[:, :-2], tp[:, 1:-1], tp[:, 2:]], axis=2)

    kb, vb = band(k), band(v)
    qi = jnp.arange(BLOCK)
    kj = jnp.arange(3 * BLOCK) - BLOCK
    rel = kj[None, :] - qi[:, None]
    bias = bias_table[rel_bucket(rel)][..., DIFF_HEADS:]
    bias = bias.transpose(2, 0, 1).reshape(WIN_KV_HEADS, WIN_GROUP, BLOCK, 3 * BLOCK)
    kpos = jnp.arange(nb)[:, None] * BLOCK + kj[None, :]
    valid = ((jnp.abs(rel) <= WINDOW)[None]
             & (kpos >= 0)[:, None, :] & (kpos < S)[:, None, :])
    s = jnp.einsum('bnqhgd,bnkhd->bnhgqk', qb, kb).astype(jnp.float32) * scale
    s = s + bias[None, None].astype(jnp.float32)
    s = jnp.where(valid[None, :, None, None], s, NEG_INF)
    sink_col = jnp.broadcast_to(
        sink.astype(jnp.float32).reshape(1, 1, WIN_KV_HEADS, WIN_GROUP, 1, 1),
        s.shape[:-1] + (1,))
    p = jax.nn.softmax(jnp.concatenate([s, sink_col], axis=-1), axis=-1)[..., :-1]
    out = jnp.einsum('bnhgqk,bnkhd->bnqhgd', p.astype(v.dtype), vb)
    return out.reshape(B, S, WIN_HEADS * HEAD_DIM)


def dwconv_centered(u, w, b):
    S = u.shape[1]
    pad = CONV_WIDTH // 2
    up = jnp.pad(u, ((0, 0), (pad, pad), (0, 0)))
    return sum(up[:, i:i + S] * w[i] for i in range(CONV_WIDTH)) + b


def encoder_layer(x, layer_idx, norm_attn_g, w_in, diff_q_norm_g, diff_k_norm_g,
                  diff_lambda_q1, diff_lambda_k1, diff_lambda_q2, diff_lambda_k2,
                  diff_subln_g, win_q_norm_g, win_k_norm_g, win_sink, rel_bias,
                  w_out, norm_ffn_g, w_gate, w_up, conv_w, conv_b, w_down):
    B, S = x.shape[0], x.shape[1]
    h = rms_norm(x, norm_attn_g)
    proj = h @ w_in
    dq, dk, dv, wq, wk, wv = jnp.split(proj, SPLIT_POINTS, axis=-1)

    dq = rms_norm(dq.reshape(B, S, DIFF_HEADS, 2, HEAD_DIM), diff_q_norm_g)
    dk = rms_norm(dk.reshape(B, S, DIFF_HEADS, 2, HEAD_DIM), diff_k_norm_g)
    dv = dv.reshape(B, S, DIFF_HEADS, 2 * HEAD_DIM)
    lam_init = 0.8 - 0.6 * math.exp(-0.3 * layer_idx)
    lam = (jnp.exp(jnp.sum(diff_lambda_q1.astype(jnp.float32) * diff_lambda_k1.astype(jnp.float32)))
           - jnp.exp(jnp.sum(diff_lambda_q2.astype(jnp.float32) * diff_lambda_k2.astype(jnp.float32)))
           + lam_init)
    o_a = diff_attention(dq, dk, dv, rel_bias, lam)
    o_a = (rms_norm(o_a, diff_subln_g) * (1.0 - lam_init)).reshape(B, S, DIFF_V_W)

    wq = rms_norm(wq.reshape(B, S, WIN_HEADS, HEAD_DIM), win_q_norm_g)
    wk = rms_norm(wk.reshape(B, S, WIN_KV_HEADS, HEAD_DIM), win_k_norm_g)
    wv = wv.reshape(B, S, WIN_KV_HEADS, HEAD_DIM)
    o_b = window_attention(wq, wk, wv, rel_bias, win_sink)

    x = x + jnp.concatenate([o_a, o_b], axis=-1) @ w_out

    h = rms_norm(x, norm_ffn_g)
    u = dwconv_centered(h @ w_gate, conv_w, conv_b)
    x = x + (jax.nn.silu(u) * (h @ w_up)) @ w_down
    return x


def setup_inputs(seed: int = 0) -> dict:
    key = jax.random.key(seed)
    ks = jax.random.split(key, 22)
    f32 = jnp.float32

    def nrm(k, shape, scale):
        return jax.random.normal(k, shape, f32) * scale

    def gain(k, shape):
        return 1.0 + 0.05 * jax.random.normal(k, shape, f32)

    return {
        'x_prompt': nrm(ks[0], (BATCH, SEQ, D_MODEL), 1.0),
        'x_sample': nrm(ks[1], (DEC_BATCH, DEC_SEQ, D_MODEL), 1.0),
        'norm_attn_g': gain(ks[2], (DEPTH, D_MODEL)),
        'w_in': nrm(ks[3], (DEPTH, D_MODEL, D_IN), D_MODEL ** -0.5),
        'diff_q_norm_g': gain(ks[4], (DEPTH, HEAD_DIM)),
        'diff_k_norm_g': gain(ks[5], (DEPTH, HEAD_DIM)),
        'diff_lambda_q1': nrm(ks[6], (DEPTH, HEAD_DIM), 0.1),
        'diff_lambda_k1': nrm(ks[7], (DEPTH, HEAD_DIM), 0.1),
        'diff_lambda_q2': nrm(ks[8], (DEPTH, HEAD_DIM), 0.1),
        'diff_lambda_k2': nrm(ks[9], (DEPTH, HEAD_DIM), 0.1),
        'diff_subln_g': gain(ks[10], (DEPTH, 2 * HEAD_DIM)),
        'win_q_norm_g': gain(ks[11], (DEPTH, HEAD_DIM)),
        'win_k_norm_g': gain(ks[12], (DEPTH, HEAD_DIM)),
        'win_sink': nrm(ks[13], (DEPTH, WIN_HEADS), 0.5),
        'rel_bias': nrm(ks[14], (N_BUCKETS, N_BIAS_HEADS), 0.5),
        'w_out': nrm(ks[15], (DEPTH, D_MIX, D_MODEL), D_MIX ** -0.5),
        'norm_ffn_g': gain(ks[16], (DEPTH, D_MODEL)),
        'w_gate': nrm(ks[17], (DEPTH, D_MODEL, D_FF), D_MODEL ** -0.5),
        'w_up': nrm(ks[18], (DEPTH, D_MODEL, D_FF), D_MODEL ** -0.5),
        'conv_w': nrm(ks[19], (DEPTH, CONV_WIDTH, D_FF), CONV_WIDTH ** -0.5),
        'conv_b': nrm(ks[20], (DEPTH, D_FF), 0.02),
        'w_down': nrm(ks[21], (DEPTH, D_FF, D_MODEL), D_FF ** -0.5),
    }


def reference(x_prompt, x_sample, norm_attn_g, w_in, diff_q_norm_g, diff_k_norm_g,
              diff_lambda_q1, diff_lambda_k1, diff_lambda_q2, diff_lambda_k2,
              diff_subln_g, win_q_norm_g, win_k_norm_g, win_sink, rel_bias,
              w_out, norm_ffn_g, w_gate, w_up, conv_w, conv_b, w_down):
    def run(x):
        for l in range(DEPTH):
            x = encoder_layer(
                x, l, norm_attn_g[l], w_in[l], diff_q_norm_g[l], diff_k_norm_g[l],
                diff_lambda_q1[l], diff_lambda_k1[l], diff_lambda_q2[l], diff_lambda_k2[l],
                diff_subln_g[l], win_q_norm_g[l], win_k_norm_g[l], win_sink[l], rel_bias,
                w_out[l], norm_ffn_g[l], w_gate[l], w_up[l], conv_w[l], conv_b[l], w_down[l])
        return x

    y_prompt = run(x_prompt)
    y_sample = run(x_sample)
    return (y_prompt, y_sample)
```

```python
import math
from contextlib import ExitStack

import numpy as np
import concourse.bass as bass
import concourse.mybir as mybir
from concourse.bass_utils import run_bass_kernel_spmd

F32 = mybir.dt.float32
BF16 = mybir.dt.bfloat16
AF = mybir.ActivationFunctionType
ALU = mybir.AluOpType
AX = mybir.AxisListType

NCORES = 8
SPC = 3
S = 2048
D = 1024
NT = S // 128
HD = 64
D_IN = 2304
D_FF = 2816
NF = D_FF // 128
EPS = 1e-6
LAM_INIT = 0.8 - 0.6 * math.exp(-0.3 * 0)
NEG = -30000.0

ENGS = ("sync", "scalar", "vector", "gpsimd", "tensor")


class Sem:
    def __init__(self, h):
        self.h = h
        self.count = 0


class Buf:
    def __init__(self, name, base=()):
        self.name = name
        self.w = list(base)
        self.r = {}


def retire(bufs):
    ev = []
    for b in bufs:
        ev.extend(b.w)
        ev.extend(b.r.values())
    return ev


class Prog:
    def __init__(self, nc, es):
        self.nc = nc
        self.es = es
        self.lists = {k: [] for k in ENGS}
        self.waited = {k: {} for k in ENGS}
        self.esem = {}
        for k in ("scalar", "vector", "gpsimd", "tensor"):
            self.esem[k] = Sem(es.enter_context(nc.semaphore("sem_" + k)))
        self.nsem = 0

    def new_sem(self):
        self.nsem += 1
        return Sem(self.es.enter_context(self.nc.semaphore("dsem%d" % self.nsem)))

    dead = False

    def wait(self, eng, ev):
        if self.dead:
            return
        sem, val = ev
        cur = self.waited[eng].get(id(sem), 0)
        if cur >= val:
            return
        self.waited[eng][id(sem)] = val
        h = sem.h
        self.lists[eng].append(lambda e: e.wait_ge(h, val))

    def _deps(self, eng, reads, writes, extra):
        for b in reads:
            for ev in b.w:
                self.wait(eng, ev)
        for b in writes:
            for ev in b.w:
                self.wait(eng, ev)
            for ev in b.r.values():
                self.wait(eng, ev)
        for ev in extra:
            self.wait(eng, ev)

    def op(self, eng, fns, reads=(), writes=(), extra=()):
        if self.dead:
            return None
        if callable(fns):
            fns = [fns]
        self._deps(eng, reads, writes, extra)
        sem = self.esem[eng]
        sem.count += 1
        h = sem.h
        for f in fns[:-1]:
            self.lists[eng].append(f)
        last = fns[-1]
        self.lists[eng].append(lambda e: last(e).then_inc(h, 1))
        ev = (sem, sem.count)
        for b in reads:
            b.r[eng] = ev
        for b in writes:
            b.w = [ev]
            b.r = {}
        return ev

    def dma(self, eng, sem, out, in_, reads=(), writes=(), extra=(), **kw):
        if self.dead:
            return None
        self._deps(eng, reads, writes, extra)
        sem.count += 16
        h = sem.h
        self.lists[eng].append(lambda e: e.dma_start(out=out, in_=in_, **kw).then_inc(h, 16))
        ev = (sem, sem.count)
        for b in reads:
            b.r["dma%d" % id(sem)] = ev
        for b in writes:
            b.w = [ev]
            b.r = {}
        return ev

    def run(self, block):
        for name in ENGS:
            lst = self.lists[name]

            def body(e, lst=lst):
                for f in lst:
                    f(e)

            getattr(block, name)(body)


def MM(out, lhsT, rhs, start=True, stop=True, skip=False):
    return lambda e: e.matmul(out, lhsT=lhsT, rhs=rhs, start=start, stop=stop, skip_group_check=skip)


def TR(out, in_, ident):
    return lambda e: e.transpose(out=out, in_=in_, identity=ident)


def ACT(out, in_, func, **kw):
    return lambda e: e.activation(out=out, in_=in_, func=func, **kw)


def TS(out, in0, s1, s2, op0, op1=None):
    if op1 is None:
        return lambda e: e.tensor_scalar(out=out, in0=in0, scalar1=s1, scalar2=None, op0=op0)
    return lambda e: e.tensor_scalar(out=out, in0=in0, scalar1=s1, scalar2=s2, op0=op0, op1=op1)


def STT(out, in0, scalar, in1, op0, op1):
    return lambda e: e.scalar_tensor_tensor(out=out, in0=in0, scalar=scalar, in1=in1, op0=op0, op1=op1)


def TT(out, in0, in1, op):
    return lambda e: e.tensor_tensor(out=out, in0=in0, in1=in1, op=op)


def TC(out, in_):
    return lambda e: e.tensor_copy(out=out, in_=in_)


def RED(out, in_, op=ALU.add, axis=AX.X):
    return lambda e: e.tensor_reduce(out=out, in_=in_, axis=axis, op=op)


def RCP(out, in_):
    return lambda e: e.reciprocal(out=out, in_=in_)


def MS(ap, val):
    return lambda e: e.memset(ap, val)


def rs(ap, shape):
    if len(shape) == 2:
        return ap
    names = "abcd"[: len(shape) - 1]
    kw = {names[i]: shape[i + 1] for i in range(len(shape) - 1)}
    return ap.rearrange("p (%s) -> p %s" % (" ".join(names), " ".join(names)), **kw)


def bc(ap, shape):
    return ap.unsqueeze(2).broadcast_to(list(shape))


def build_nc(spc=SPC, stop="E", dbg=False, info=None):
    try:
        return _build_nc(spc, stop, dbg, info)
    except Exception as ex:
        if type(ex).__name__ == '_Cut':
            return _LAST[0]
        raise


_LAST = [None]


def _build_nc(spc=SPC, stop="E", dbg=False, info=None):
    nc = bass.Bass("TRN2", target_bir_lowering=False)
    _LAST[0] = nc
    stop_i = "0ABCDE".index(stop)

    def din(name, shape, dt=F32):
        return nc.dram_tensor(name, list(shape), dt, kind="ExternalInput").ap()

    xin = din("xin", [SPC, S, D])
    w_in = din("w_in", [D, D_IN])
    w_out = din("w_out", [D, D])
    w_gate = din("w_gate", [D, D_FF])
    w_up = din("w_up", [D, D_FF])
    w_down = din("w_down", [D_FF, D])
    gA_d = din("gA", [1, D])
    gF_d = din("gF", [1, D])
    gcols_d = din("gcols", [128, 4])
    lamv_d = din("lamv", [1, 256])
    gsub_d = din("gsub", [1, 128])
    sink_d = din("sink", [1, 8])
    cfar_d = din("cfar", [1, 8])
    bandd_d = din("bandd", [128, 4 * 384])
    bandw_d = din("bandw", [128, 2 * 1536])
    maskw_d = din("maskw", [128, 384])
    cw_d = din("cw", [128, NF * 3])
    cb_d = din("cb", [128, NF])
    yout = nc.dram_tensor("yout", [SPC, S, D], F32, kind="ExternalOutput").ap()
    dbg_d = nc.dram_tensor("dbg", [128, 52480], F32, kind="ExternalOutput").ap() if dbg else None

    w_in_b = nc.dram_tensor("w_in_b", [D, D_IN], BF16, kind="Internal").ap()
    w_out_b = nc.dram_tensor("w_out_b", [D, D], BF16, kind="Internal").ap()
    wgu_b = nc.dram_tensor("wgu_b", [NF, 128, 2, 8, 128], BF16, kind="Internal").ap()
    w_down_b = nc.dram_tensor("w_down_b", [D_FF, D], BF16, kind="Internal").ap()

    with ExitStack() as es:
        AW = 52480
        arena = es.enter_context(nc.sbuf_tensor("arena", [128, AW], F32))
        psum = es.enter_context(nc.psum_tensor("psum", [128, 4096], F32))
        P = Prog(nc, es)
        block = es.enter_context(nc.Block())
        import os as _os
        CUT = int(_os.environ.get("K_CUT", "0"))

        class _Cut(Exception):
            pass

        def cut(n):
            if CUT == n:
                s_c = P.new_sem()
                ev_ = P.dma("gpsimd", s_c, yout[0, 0:1, 0:4], xin[0, 0:1, 0:4])
                P.wait("gpsimd", ev_)
                for k_ in ("scalar", "vector", "gpsimd", "tensor"):
                    if P.esem[k_].count:
                        P.wait("gpsimd", (P.esem[k_], P.esem[k_].count))
                P.dead = True

        def V(off, nwords, shape, dt=F32):
            a = arena[:, off:off + nwords]
            if dt != F32:
                a = a.bitcast(dt)
            return rs(a, shape)

        def PS(bank0, nbanks, shape=None, dt=F32):
            a = psum[:, bank0 * 512:(bank0 + nbanks) * 512]
            if dt != F32:
                a = a.bitcast(dt)
            return a if shape is None else rs(a, shape)

        PB = [Buf("bank%d" % i) for i in range(8)]

        cur = [0]

        def alloc(n):
            o = cur[0]
            cur[0] += n
            return o

        o_ident = alloc(64)
        o_gA = alloc(1024)
        o_gF = alloc(1024)
        o_gsub = alloc(128)
        o_gcols = alloc(4)
        o_gcs = alloc(4)
        o_lamv = alloc(256)
        o_lamt = alloc(128)
        o_sm = alloc(64)
        o_mhalf = alloc(32)
        o_eps = alloc(2)
        o_zero = alloc(64)
        o_esink = alloc(8)
        o_cfar = alloc(8)
        o_cw = alloc(NF * 3)
        o_cb = alloc(NF)
        o_mask = alloc(384)
        o_stat = alloc(512)
        o_xs = [alloc(1024), alloc(1024)]
        o_hb = [alloc(512), alloc(512)]
        o_junk = alloc(512)
        o_sqf = alloc(1664)
        HTW = 2052
        o_hT = alloc(8 * HTW // 2)
        o_X = cur[0]
        XW = 19840
        cur[0] += XW
        o_Z = cur[0]
        ZW = 10752
        cur[0] += ZW
        o_Z3 = cur[0]
        Z3W = 4608
        cur[0] += Z3W
        assert cur[0] <= AW, cur[0]

        ident = V(o_ident, 64, [128, 128], BF16)
        gA_b = V(o_gA, 1024, [128, 1024])
        gF_b = V(o_gF, 1024, [128, 1024])
        gsub_b = V(o_gsub, 128, [128, 128])
        gcols = V(o_gcols, 4, [128, 4])
        gcs = V(o_gcs, 4, [128, 4])
        lamv = V(o_lamv, 256, [128, 256])
        lamt = V(o_lamt, 128, [128, 128])
        sm = V(o_sm, 64, [128, 64])
        mhalf = V(o_mhalf, 32, [128, 32])
        epsc = V(o_eps, 2, [128, 2])[:, 0:1]
        zero_b = V(o_zero, 64, [128, 128], BF16)
        esink = V(o_esink, 8, [128, 8])
        cfar = V(o_cfar, 8, [128, 8])
        cw = V(o_cw, NF * 3, [128, NF, 3])
        cb = V(o_cb, NF, [128, NF])
        maskw = V(o_mask, 384, [128, 3, 128])
        stat = V(o_stat, 512, [128, 512])
        xs = [V(o, 1024, [128, 1024]) for o in o_xs]
        hb = [V(o, 512, [128, 1024], BF16) for o in o_hb]
        junk = V(o_junk, 512, [128, 1024], BF16)
        sqf = V(o_sqf, 1664, [128, 1664])
        hT = V(o_hT, 8 * HTW // 2, [128, 8, HTW], BF16)

        oX = o_X
        qT_d = V(oX, 4096, [128, 4, S], BF16); oX += 4096
        kT_d = V(oX, 4096, [128, 4, S], BF16); oX += 4096
        v_d = V(oX, 16 * 4 * 65, [128, NT, 4, 130], BF16); oX += 16 * 4 * 65
        qT_w = V(oX, 4096, [128, 4, S], BF16); oX += 4096
        kT_w = V(oX, 2048, [128, 2, S], BF16); oX += 2048
        v_w = V(oX, 16 * 2 * 33, [128, NT, 2, 66], BF16); oX += 16 * 2 * 33
        assert oX - o_X <= XW, oX - o_X
        x2res = V(o_X, 16384, [128, NT, D])
        aT = [V(o_X + 16384 + i * 512, 512, [128, 8, 128], BF16) for i in range(2)]
        w_in_sb = V(o_Z, 8 * 1152, [128, 8, D_IN], BF16)
        attn_tok = V(o_Z, 8192, [128, NT, D], BF16)
        actT = V(o_Z, NF * 256, [128, NF, 512], BF16)
        oZ = o_Z + NF * 256
        wgu_sb = [V(oZ + i * 1024, 1024, [128, 2, 8, 128], BF16) for i in range(3)]
        oZ += 3 * 1024
        NWD = 4
        wd_sb = [V(oZ + i * 512, 512, [128, 1024], BF16) for i in range(NWD)]
        oZ += NWD * 512
        assert oZ - o_Z <= ZW, oZ - o_Z
        raw = V(o_Z3, 2304, [128, 2304])
        qn2 = [V(o_Z3 + 2304 + i * 896, 896, [128, 1792], BF16) for i in range(2)]
        et = [V(o_Z3 + i * 512, 512, [128, 1024], BF16) for i in range(3)]
        ostg = [V(o_sqf + i * 512, 512, [128, 512]) for i in range(3)]
        adjb = [[V(o_Z3 + 1536 + (ty * 2 + hl) * 768, 768, [128, 4, 384], BF16) for hl in range(2)] for ty in range(2)]
        bandf = V(o_hT, 1536, [128, 4, 384])
        tmpf = V(o_hT + 1536, 384, [128, 384])
        etw = [V(o_hT + 5120 + i * 384, 384, [128, 768], BF16) for i in range(3)]
        bandw = V(o_hT + 2048, 3072, [128, 4, 768])
        w_out_sb = V(o_Z3, 4096, [128, 8, D], BF16)
        ustg = [[V(o_Z3 + (j * 3 + i) * 512, 512, [128, 512]) for i in range(3)] for j in range(2)]
        assert 3072 + 1536 <= Z3W and 1536 + 3072 <= Z3W
        ystg = [V(o_Z3 + 3072, 1024, [128, 1024]), V(o_sqf, 1024, [128, 1024])]

        s_init = P.new_sem()
        s_x = [P.new_sem(), P.new_sem()]
        s_y = [P.new_sem(), P.new_sem()]
        s_win4 = [P.new_sem() for _ in range(4)]
        s_wout = P.new_sem()
        s_band = P.new_sem()
        s_bandw = P.new_sem()
        s_wgu = [P.new_sem() for _ in range(3)]
        s_wd = [P.new_sem() for _ in range(NWD + 6)]
        s_cv = {k: P.new_sem() for k in ("w_in", "w_out", "wgu", "w_down")}

        cut(1)
        B_const = Buf("const")
        init_loads = [
            (gA_b, gA_d.partition_broadcast(128)),
            (gF_b, gF_d.partition_broadcast(128)),
            (gsub_b, gsub_d.partition_broadcast(128)),
            (gcols, gcols_d),
            (lamv, lamv_d.partition_broadcast(128)),
            (esink, sink_d.partition_broadcast(128)),
            (cfar, cfar_d.partition_broadcast(128)),
            (V(o_cw, NF * 3, [128, NF * 3]), cw_d),
            (cb, cb_d),
            (V(o_mask, 384, [128, 384]), maskw_d),
        ]
        import os as _os
        LVL = int(_os.environ.get("K_LVL", "99"))
        for o_, i_ in init_loads[:LVL]:
            P.dma("sync", s_init, o_, i_)
        B_const.w = [(s_init, s_init.count)]

        cut(2)
        B_winb = Buf("w_in_b"); B_woutb = Buf("w_out_b"); B_wgub = Buf("wgu_b"); B_wdb = Buf("w_down_b")

        def conv_w_in():
            P.dma("gpsimd", s_cv["w_in"], w_in_b.rearrange("d (a c) -> (d a) c", a=2),
                  w_in.rearrange("d (a c) -> (d a) c", a=2))
            B_winb.w = [(s_cv["w_in"], s_cv["w_in"].count)]

        def conv_rest():
            P.dma("gpsimd", s_cv["w_out"], w_out_b, w_out)
            B_woutb.w = [(s_cv["w_out"], s_cv["w_out"].count)]
            for m, wsrc in enumerate((w_gate, w_up)):
                for k in range(8):
                    P.dma("gpsimd", s_cv["wgu"],
                          wgu_b[:, :, m, k, :].rearrange("c p f -> p c f"),
                          wsrc[k * 128:(k + 1) * 128, :].rearrange("p (c f) -> p c f", f=128))
            B_wgub.w = [(s_cv["wgu"], s_cv["wgu"].count)]
            P.dma("gpsimd", s_cv["w_down"], w_down_b, w_down)
            B_wdb.w = [(s_cv["w_down"], s_cv["w_down"].count)]

        import os as _os
        if not _os.environ.get('K_SKIP_CONV'):
            conv_w_in()

        cut(3)
        B_ident = Buf("ident"); B_mhalf = Buf("mhalf"); B_lamt = Buf("lamt"); B_sm = Buf("sm")
        idf = V(o_lamt, 128, [128, 128])
        P.op("gpsimd", MS(idf, 0.0), writes=[B_lamt])
        P.op("gpsimd", lambda e: e.affine_select(out=idf, in_=idf, pattern=[[-1, 128]], compare_op=ALU.not_equal,
                                                 fill=1.0, base=0, channel_multiplier=1),
             reads=[B_lamt], writes=[B_lamt])
        P.op("vector", TC(ident, idf), reads=[B_lamt], writes=[B_ident])
        P.op("gpsimd", [MS(mhalf, -0.5), MS(V(o_eps, 2, [128, 2]), EPS)], writes=[B_mhalf])
        B_zero = Buf("zero")
        P.op("gpsimd", MS(zero_b, 0.0), writes=[B_zero])
        B_hT = [Buf("hT%d" % t) for t in range(NT)]
        B_hTpad = Buf("hTpad")

        cut(4)
        P.op("vector", TT(lamt[:, 0:64], lamv[:, 0:64], lamv[:, 64:128], ALU.mult), reads=[B_const, B_ident], writes=[B_lamt])
        P.op("vector", TT(lamt[:, 64:128], lamv[:, 128:192], lamv[:, 192:256], ALU.mult), reads=[B_const], writes=[B_lamt])
        P.op("vector", RED(sm[:, 0:2], rs(lamt, [128, 2, 64])), reads=[B_lamt], writes=[B_sm])
        P.op("scalar", ACT(sm[:, 2:4], sm[:, 0:2], AF.Exp), reads=[B_sm], writes=[B_sm])
        P.op("vector", TS(sm[:, 4:5], sm[:, 3:4], sm[:, 2:3], -LAM_INIT, ALU.subtract, ALU.add), reads=[B_sm], writes=[B_sm])
        negl = sm[:, 4:5]
        B_gc = Buf("gconst")
        P.op("vector", TC(gcs, gcols), reads=[B_const], writes=[B_gc])
        P.op("vector", TS(gcs[:, 0:3:2], gcs[:, 0:3:2], 0.125, None, ALU.mult), reads=[B_gc], writes=[B_gc])
        P.op("vector", TS(gsub_b, gsub_b, 1.0 - LAM_INIT, None, ALU.mult), reads=[B_const], writes=[B_gc])
        P.op("scalar", ACT(esink, esink, AF.Exp), reads=[B_const], writes=[B_gc])
        CONST = [B_const, B_gc, B_sm]

        cut(5)
        class Region:
            def __init__(self):
                self.bufs = []
                self.base = []

            def switch(self, names):
                self.base = retire(self.bufs) + list(self.base)
                best = {}
                for s_, v_ in self.base:
                    if id(s_) not in best or best[id(s_)][1] < v_:
                        best[id(s_)] = (s_, v_)
                self.base = list(best.values())
                self.bufs = [Buf(n, self.base) for n in names]
                return self.bufs

        RX, RZ, RZ3 = Region(), Region(), Region()
        B_xs = [Buf("xs0"), Buf("xs1")]
        B_hb = [Buf("hb0"), Buf("hb1")]
        B_junk = Buf("junk"); B_sqf = Buf("sqf")
        B_stat = [Buf("stat%d" % i) for i in range(8)]

        def rsqrt_pool(out, in_, n, inv_n, tmp, reads, writes):
            P.op("gpsimd", TS(tmp, in_, inv_n, EPS, ALU.mult, ALU.add), reads=reads, writes=writes)
            P.op("gpsimd", TT(out, tmp, mhalf[:, 0:n], ALU.pow), reads=writes + [B_mhalf], writes=writes)

        def rsqrt_act(out, in_, inv_n, tmp, reads, writes):
            P.op("scalar", ACT(tmp, in_, AF.Sqrt, scale=inv_n, bias=EPS), reads=reads, writes=writes)
            P.op("vector", RCP(out, tmp), reads=writes, writes=writes)

        def stcol(slot, j, n=1):
            return stat[:, slot * 64 + j: slot * 64 + j + n]

        for s in range(spc):
            if stop_i < 1:
                continue
            B_qTd, B_kTd, B_vd, B_qTw, B_kTw, B_vw = RX.switch(["qTd", "kTd", "vd", "qTw", "kTw", "vw"])
            B_win4 = RZ.switch(["w_in0", "w_in1", "w_in2", "w_in3"])
            B_raw, B_qn0, B_qn1 = RZ3.switch(["raw", "qn0", "qn1"])
            B_qn2 = [B_qn0, B_qn1]
            P.op("gpsimd", MS(v_d[:, :, :, 128:130], 1.0), writes=[B_vd])
            P.op("gpsimd", MS(v_w[:, :, :, 64:66], 1.0), writes=[B_vw])

            WIN_DEPS = [[B_win4[0]], [B_win4[0]], [B_win4[1]], [B_win4[1], B_win4[2]], [B_win4[2], B_win4[3]]]

            def A_S1a(t):
                sl = t % 2
                P.dma("sync", s_x[sl], xs[sl], xin[s, t * 128:(t + 1) * 128, :], writes=[B_xs[sl]])
                st_ = B_stat[sl]
                P.op("scalar", ACT(junk, xs[sl], AF.Square, accum_out=stcol(sl, 0)), reads=[B_xs[sl]], writes=[B_junk, st_])
                rsqrt_act(stcol(sl, 1), stcol(sl, 0), 1.0 / D, stcol(sl, 2), [st_], [st_])
                P.op("vector", STT(hb[sl], xs[sl], stcol(sl, 1), gA_b, ALU.mult, ALU.mult),
                     reads=[B_xs[sl], st_] + CONST, writes=[B_hb[sl]])

            def A_S1b(t):
                sl = t % 2
                tc0 = 2 + t * 128
                pT = PS(0, 1, None, BF16)
                P.op("tensor", [TR(pT[:, c * 128:(c + 1) * 128], hb[sl][:, c * 128:(c + 1) * 128], ident) for c in range(8)],
                     reads=[B_hb[sl], B_ident], writes=[PB[0]])
                P.op("scalar", ACT(hT[:, :, tc0:tc0 + 128], rs(pT, [128, 8, 128]), AF.Copy), writes=[B_hT[t], PB[0]])

            def A_S2(t):
                tc0 = 2 + t * 128
                psA = PS(1, 5)
                for cc in range(5):
                    n = 512 if cc < 4 else 256
                    P.op("tensor", [MM(psA[:, cc * 512:cc * 512 + n], hT[:, k, tc0:tc0 + 128],
                                       w_in_sb[:, k, cc * 512:cc * 512 + n], start=(k == 0), stop=(k == 7)) for k in range(8)],
                         reads=[B_hT[t]] + WIN_DEPS[cc], writes=[PB[1 + cc]])

            def A_S3a(t):
                psA = PS(1, 5)
                P.op("scalar", ACT(raw[:, 0:1536], psA[:, 0:1536], AF.Copy), writes=[B_raw, PB[1], PB[2], PB[3]])
                P.op("vector", TC(raw[:, 1536:1664], psA[:, 1536:1664]), writes=[B_raw, PB[4]])
                P.op("vector", TC(v_d[:, t, :, 0:128], rs(psA[:, 1664:2176], [128, 4, 128])), writes=[B_vd, PB[4], PB[5]])
                P.op("vector", TC(v_w[:, t, :, 0:64], rs(psA[:, 2176:2304], [128, 2, 64])), writes=[B_vw, PB[5]])

            def A_S3b(t):
                sl = 2 + (t % 2)
                st_ = B_stat[sl]
                qs = t % 2
                qn_ = qn2[qs]
                P.op("scalar", ACT(sqf, raw[:, 0:1664], AF.Square), reads=[B_raw], writes=[B_sqf])
                P.op("vector", RED(stcol(sl, 0, 26), rs(sqf, [128, 26, 64])), reads=[B_sqf], writes=[st_])
                rsqrt_act(stcol(sl, 32, 26), stcol(sl, 0, 26), 1.0 / HD, stcol(sl, 0, 26), [st_], [st_])
                rq = stcol(sl, 32, 26)
                P.op("vector", TT(rs(qn_[:, 0:1536], [128, 24, 64]), rs(raw[:, 0:1536], [128, 24, 64]),
                                  bc(rq[:, 0:24], [128, 24, 64]), ALU.mult), reads=[B_raw, st_], writes=[B_qn2[qs]])
                qk = rs(qn_[:, 1536:1792], [128, 2, 128])
                for dup in range(2):
                    P.op("vector", TT(qk[:, :, dup * 64:(dup + 1) * 64], rs(raw[:, 1536:1664], [128, 2, 64]),
                                      bc(rq[:, 24:26], [128, 2, 64]), ALU.mult), reads=[B_raw, st_], writes=[B_qn2[qs]])

            def A_S4(t):
                pQ = PS(6, 2, None, BF16)
                tcs = slice(t * 128, (t + 1) * 128)
                qs = t % 2
                qn_ = qn2[qs]
                P.op("tensor", [TR(pQ[:, j * 128:(j + 1) * 128], qn_[:, j * 128:(j + 1) * 128], ident) for j in range(14)],
                     reads=[B_qn2[qs], B_ident], writes=[PB[6], PB[7]])
                P.op("scalar", ACT(qT_d[:, :, tcs], rs(pQ[:, 0:512], [128, 4, 128]), AF.Identity, scale=gcs[:, 0:1]),
                     reads=CONST, writes=[B_qTd, PB[6]])
                P.op("scalar", ACT(kT_d[:, :, tcs], rs(pQ[:, 512:1024], [128, 4, 128]), AF.Identity, scale=gcs[:, 1:2]),
                     reads=CONST, writes=[B_kTd, PB[6]])
                P.op("scalar", ACT(qT_w[:, :, tcs], rs(pQ[:, 1024:1536], [128, 4, 128]), AF.Identity, scale=gcs[:, 2:3]),
                     reads=CONST, writes=[B_qTw, PB[7]])
                P.op("vector", TS(kT_w[:, :, tcs], rs(pQ[:, 1536:1792], [128, 2, 128]), gcs[:, 3:4], None, ALU.mult),
                     reads=CONST, writes=[B_kTw, PB[7]])

            def load_w_in():
                wv_ = w_in_b.rearrange("(k p) c -> p k c", p=128)
                for gi, (d0, d1, s0) in enumerate(((0, 1024, 0), (1536, 2176, 1024), (1024, 1536, 1664), (2176, 2304, 2176))):
                    P.dma("sync", s_win4[gi], w_in_sb[:, :, s0:s0 + (d1 - d0)], wv_[:, :, d0:d1], reads=[B_winb], writes=[B_win4[gi]])

            if s > 0:
                load_w_in()
            A_S1a(0)
            A_S1a(1)
            A_S1b(0)
            A_S1b(1)
            A_S1a(2)
            NPRE = NT if s == 0 else 3
            for t in range(2, NPRE):
                A_S1b(t)
                if t + 1 < NT:
                    A_S1a(t + 1)
            if s == 0:
                load_w_in()
            for t in range(NT):
                A_S2(t)
                A_S3a(t)
                if NPRE <= t + 2 < NT:
                    A_S1b(t + 2)
                if NPRE <= t + 2 and t + 3 < NT:
                    A_S1a(t + 3)
                if t >= 1:
                    A_S4(t - 1)
                A_S3b(t)
            A_S4(NT - 1)
            if s == 0:
                P.wait("gpsimd", (P.esem["tensor"], P.esem["tensor"].count))
                conv_rest()

            if stop_i < 2:
                continue
            (B_attn,) = RZ.switch(["attn_tok"])
            zb3 = RZ3.switch(["et0", "et1", "et2", "adj0", "adj1", "adj2", "adj3"])
            B_et = zb3[0:3]
            B_adj = zb3[3:7]
            base_sq = retire([B_sqf])
            B_o0, B_o1, B_o2 = Buf("o0", base_sq), Buf("o1", base_sq), Buf("o2", base_sq)
            base_h = retire(B_hT + [B_hTpad])
            B_bandf, B_tmpf = Buf("bandf", base_h), Buf("tmpf", base_h)
            P.dma("sync", s_band, V(o_hT, 1536, [128, 1536]), bandd_d, writes=[B_bandf])
            B_bandw = Buf("bandw", base_h)
            B_etw = [Buf("etw%d" % i_, base_h) for i_ in range(3)]
            P.dma("sync", s_bandw, V(o_hT + 2048, 3072, [128, 3072]), bandw_d, writes=[B_bandw])
            for h_ in range(4):
                for ty in range(2):
                    P.op("vector", TS(tmpf, bandf[:, h_, :], cfar[:, ty * 4 + h_:ty * 4 + h_ + 1], None, ALU.subtract),
                         reads=[B_bandf] + CONST, writes=[B_tmpf])
                    P.op("vector", TC(adjb[ty][0][:, h_, :], tmpf), reads=[B_tmpf], writes=[B_adj[h_]])
                    P.op("vector", TT(adjb[ty][1][:, h_, :], tmpf, adjb[ty][0][:, h_, :], ALU.subtract),
                         reads=[B_tmpf, B_adj[h_]], writes=[B_adj[h_]])
            stB = [PS(0, 2), PS(2, 2)]
            stBB = [[PB[0], PB[1]], [PB[2], PB[3]]]
            acc = PS(4, 4)
            accB = [PB[4], PB[5], PB[6], PB[7]]
            tiles = [(h, qc, kb) for h in range(4) for qc in range(4) for kb in range(16)]

            def B_QK(n):
                h, qc, kb = tiles[n]
                b = n % 2
                kc = slice(kb * 128, (kb + 1) * 128)
                qcs = slice(qc * 512, (qc + 1) * 512)
                fns = [MM(stB[b][:, 0:512], kT_d[0:64, h, kc], qT_d[0:64, h, qcs]),
                       MM(stB[b][:, 512:1024], kT_d[64:128, h, kc], qT_d[64:128, h, qcs])]
                rd = [B_kTd, B_qTd]
                lo = max(4 * qc, kb - 1)
                hi = min(4 * qc + 3, kb + 1)
                far31 = (4 * qc < kb - 1)
                ty = 1 if far31 else 0
                cf = cfar[:, ty * 4 + h:ty * 4 + h + 1]
                if lo <= hi:
                    ncol = (hi - lo + 1) * 128
                    c0 = (lo - 4 * qc) * 128
                    j0 = (lo - kb + 1) * 128
                    for m in range(2):
                        for hl in range(2):
                            fns.append(MM(stB[b][:, m * 512 + c0:m * 512 + c0 + ncol], ident,
                                          adjb[ty][hl][:, h, j0:j0 + ncol], start=False, stop=(hl == 1), skip=True))
                    rd = rd + [B_adj[h], B_ident]
                P.op("tensor", fns, reads=rd, writes=stBB[b])
                e_ = n % 3
                P.op("scalar", ACT(et[e_], stB[b], AF.Exp, bias=cf), reads=CONST, writes=[B_et[e_]] + stBB[b])

            def B_PV(n):
                h, qc, kb = tiles[n]
                e_ = n % 3
                fns = []
                for j in range(4):
                    for m in range(2):
                        fns.append(MM(acc[:, j * 512 + m * 256: j * 512 + m * 256 + 129],
                                      et[e_][:, m * 512 + j * 128: m * 512 + (j + 1) * 128],
                                      v_d[:, kb, h, 0:129], start=(kb == 0 and m == 0), stop=(kb == 15), skip=True))
                P.op("tensor", fns, reads=[B_et[e_], B_vd], writes=accB)

            def B_EPI1(h, qc):
                st_ = B_stat[4]
                a3 = rs(acc, [128, 4, 512])
                r = rs(stcol(4, 0, 8), [128, 4, 2])
                P.op("vector", RCP(r, rs(acc, [128, 4, 2, 256])[:, :, :, 128]),
                     writes=[st_] + accB)
                P.op("vector", TS(stcol(4, 8, 4), r[:, :, 1], negl, None, ALU.mult), reads=[st_] + CONST, writes=[st_])
                o1 = rs(ostg[0], [128, 4, 128]); o2 = rs(ostg[1], [128, 4, 128])
                P.op("vector", TT(o1, a3[:, :, 0:128], r[:, :, 0:1].broadcast_to([128, 4, 128]), ALU.mult),
                     reads=[st_], writes=[B_o0] + accB)
                P.op("vector", TT(o2, a3[:, :, 256:384], bc(stcol(4, 8, 4), [128, 4, 128]), ALU.mult),
                     reads=[st_], writes=[B_o1] + accB)

            def B_EPI2a(h, qc):
                st_ = B_stat[7]
                o3 = rs(ostg[2], [128, 4, 128])
                P.op("vector", TT(ostg[0], ostg[0], ostg[1], ALU.add), reads=[B_o1], writes=[B_o0])
                P.op("vector", TT(ostg[2], ostg[0], ostg[0], ALU.mult), reads=[B_o0], writes=[B_o2])
                P.op("vector", RED(stcol(7, 12, 4), o3), reads=[B_o2], writes=[st_])
                P.op("scalar", ACT(stcol(7, 20, 4), stcol(7, 12, 4), AF.Ln, scale=1.0 / 128, bias=epsc), reads=[st_, B_mhalf], writes=[st_])
                P.op("scalar", ACT(stcol(7, 16, 4), stcol(7, 20, 4), AF.Exp, scale=-0.5), reads=[st_], writes=[st_])

            def B_EPI2b(h, qc):
                st_ = B_stat[7]
                o1 = rs(ostg[0], [128, 4, 128])
                for j in range(4):
                    qb = 4 * qc + j
                    P.op("vector", STT(attn_tok[:, qb, h * 128:(h + 1) * 128], o1[:, j, :], stcol(7, 16 + j), gsub_b,
                                       ALU.mult, ALU.mult), reads=[B_o0, st_] + CONST, writes=[B_attn])

            NB = len(tiles)
            B_QK(0)
            B_QK(1)
            NWARM = int(_os.environ.get("K_WARM", "5"))
            pend = None
            for n in range(NB):
                B_PV(n)
                if n + 2 < NB:
                    B_QK(n + 2)
                h, qc, kb = tiles[n]
                if kb == 1 and pend is not None:
                    B_EPI2a(*pend)
                if kb == 5 and pend is not None:
                    B_EPI2b(*pend)
                    pend = None
                if kb == 15:
                    B_EPI1(h, qc)
                    pend = (h, qc)
                elif NWARM and not (max(4 * qc, kb - 1) <= min(4 * qc + 3, kb + 1)):
                    P.wait("tensor", B_zero.w[0])
                    for _w in range(NWARM):
                        P.lists["tensor"].append(MM(acc[:, 386:512], zero_b, v_d[:, 0, 0, 0:126], start=False, stop=False, skip=True))

            if stop_i < 3:
                continue
            if pend is not None:
                B_EPI2a(*pend)
                B_EPI2b(*pend)
            B_sqf.w = B_sqf.w + retire([B_o0, B_o1, B_o2])
            (B_wout,) = RZ3.switch(["w_out_sb"])
            P.dma("sync", s_wout, w_out_sb, w_out_b.rearrange("(k p) n -> p k n", p=128), reads=[B_woutb], writes=[B_wout])
            for gp in range(4):
                P.op("vector", TT(rs(bandw[:, gp, :], [128, 3, 2, 128]), rs(bandw[:, gp, :], [128, 3, 2, 128]),
                                  maskw.unsqueeze(2).broadcast_to([128, 3, 2, 128]), ALU.add),
                     reads=[B_bandw, B_const], writes=[B_bandw])
            stW = [PS(0, 2), PS(2, 2), PS(4, 2)]
            stWB = [[PB[0], PB[1]], [PB[2], PB[3]], [PB[4], PB[5]]]
            accW = [PS(6, 1), PS(7, 1)]
            wt = [(g, qb, ph) for g in range(2) for qb in range(NT) for ph in range(2)]

            def C_kks(qb):
                return [kk for kk in range(3) if 0 <= qb + kk - 1 < NT]

            def C_QK(m):
                g, qb, ph = wt[m]
                b = m % 3
                kks = C_kks(qb)
                fns = []
                for kk in kks:
                    kb = qb + kk - 1
                    for j in range(2):
                        hq = 4 * g + 2 * j + ph
                        fns.append(MM(stW[b][:, (kk * 2 + j) * 128:(kk * 2 + j + 1) * 128],
                                      kT_w[ph * 64:(ph + 1) * 64, g, kb * 128:(kb + 1) * 128],
                                      qT_w[ph * 64:(ph + 1) * 64, hq // 2, qb * 128:(qb + 1) * 128]))
                P.op("tensor", fns, reads=[B_kTw, B_qTw], writes=stWB[b])
                c0, c1 = kks[0] * 256, (kks[-1] + 1) * 256
                P.op("vector", TT(stW[b][:, c0:c1], stW[b][:, c0:c1], bandw[:, g * 2 + ph, c0:c1], ALU.add),
                     reads=[B_bandw], writes=stWB[b])
                P.op("scalar", ACT(etw[b][:, c0:c1], stW[b][:, c0:c1], AF.Exp), writes=[B_etw[b]] + stWB[b])

            def C_PV(m):
                g, qb, ph = wt[m]
                b = m % 3
                a = (g * NT + qb) % 2
                kks = C_kks(qb)
                fns = []
                first = (ph == 0)
                for j in range(2):
                    i = 2 * j + ph
                    for kk in kks:
                        kb = qb + kk - 1
                        fns.append(MM(accW[a][:, i * 128:i * 128 + 65],
                                      etw[b][:, (kk * 2 + j) * 128:(kk * 2 + j + 1) * 128],
                                      v_w[:, kb, g, 0:65], start=first, stop=(kk == kks[-1]), skip=True))
                        first = False
                P.op("tensor", fns, reads=[B_etw[b], B_vw], writes=[PB[6 + a]])
                if ph == 0:
                    return
                st_ = B_stat[5 + a]
                sl = 5 + a
                a3 = rs(accW[a], [128, 4, 128])
                P.op("vector", TT(stcol(sl, 0, 4), a3[:, :, 64], esink[:, 4 * g:4 * g + 4], ALU.add),
                     reads=CONST, writes=[st_, PB[6 + a]])
                P.op("vector", RCP(stcol(sl, 4, 4), stcol(sl, 0, 4)), reads=[st_], writes=[st_])
                P.op("vector", TT(rs(attn_tok[:, qb, 512 + g * 256:512 + (g + 1) * 256], [128, 4, 64]), a3[:, :, 0:64],
                                  bc(stcol(sl, 4, 4), [128, 4, 64]), ALU.mult),
                     reads=[st_], writes=[B_attn, PB[6 + a]])

            NW = len(wt)
            C_QK(0)
            C_QK(1)
            for m in range(NW):
                if m + 2 < NW:
                    C_QK(m + 2)
                C_PV(m)

            ev_h = retire([B_bandf, B_tmpf, B_bandw] + B_etw)
            for b_ in B_hT + [B_hTpad]:
                b_.w = b_.w + ev_h
            P.op("gpsimd", [MS(hT[:, :, 0:2], 0.0), MS(hT[:, :, 2050:2052], 0.0)], writes=[B_hTpad])
            if stop_i < 4:
                continue
            B_x2, B_aT0, B_aT1 = RX.switch(["x2res", "aT0", "aT1"])
            B_aT = [B_aT0, B_aT1]
            psD = [PS(1, 2), PS(4, 2)]
            psDB = [[PB[1], PB[2]], [PB[4], PB[5]]]

            def D_S1(t):
                sl = t % 2
                pT = PS(0, 1, None, BF16)
                P.op("tensor", [TR(pT[:, c * 128:(c + 1) * 128], attn_tok[:, t, c * 128:(c + 1) * 128], ident) for c in range(8)],
                     reads=[B_attn, B_ident], writes=[PB[0]])
                P.op("vector", TC(aT[sl], rs(pT, [128, 8, 128])), writes=[B_aT[sl], PB[0]])
                P.dma("sync", s_x[sl], xs[sl], xin[s, t * 128:(t + 1) * 128, :], writes=[B_xs[sl]])

            def D_S2(t):
                sl = t % 2
                for cc in range(2):
                    P.op("tensor", [MM(psD[sl][:, cc * 512:(cc + 1) * 512], aT[sl][:, c, :],
                                       w_out_sb[:, c, cc * 512:(cc + 1) * 512], start=(c == 0), stop=(c == 7)) for c in range(8)],
                         reads=[B_aT[sl], B_wout], writes=[psDB[sl][cc]])

            def D_S3(t):
                sl = t % 2
                st_ = B_stat[sl]
                P.op("vector", TT(x2res[:, t, :], psD[sl], xs[sl], ALU.add), reads=[B_xs[sl]], writes=[B_x2] + psDB[sl])
                P.op("scalar", ACT(junk, x2res[:, t, :], AF.Square, accum_out=stcol(sl, 0)), reads=[B_x2], writes=[B_junk, st_])
                rsqrt_act(stcol(sl, 1), stcol(sl, 0), 1.0 / D, stcol(sl, 2), [st_], [st_])
                P.op("vector", STT(hb[sl], x2res[:, t, :], stcol(sl, 1), gF_b, ALU.mult, ALU.mult),
                     reads=[B_x2, st_] + CONST, writes=[B_hb[sl]])

            def D_S4(t):
                sl = t % 2
                tc0 = 2 + t * 128
                pT2 = PS(3, 1, None, BF16)
                P.op("tensor", [TR(pT2[:, c * 128:(c + 1) * 128], hb[sl][:, c * 128:(c + 1) * 128], ident) for c in range(8)],
                     reads=[B_hb[sl], B_ident], writes=[PB[3]])
                P.op("scalar", ACT(hT[:, :, tc0:tc0 + 128], rs(pT2, [128, 8, 128]), AF.Copy), writes=[B_hT[t], PB[3]])

            D_S1(0)
            for t in range(NT):
                if t + 1 < NT:
                    D_S1(t + 1)
                D_S2(t)
                D_S3(t)
                if t >= 1:
                    D_S4(t - 1)
            D_S4(NT - 1)

            if stop_i < 5:
                continue
            zb = RZ.switch(["actT"] + ["wgu%d" % i for i in range(3)] + ["wd%d" % i for i in range(NWD)])
            B_actT = zb[0]; B_wgu = zb[1:4]; B_wd = list(zb[4:4 + NWD])
            ub = RZ3.switch(["u%d%d" % (j, i) for j in range(2) for i in range(3)] + ["y0"])
            B_u = [ub[0:3], ub[3:6]]
            B_y = [ub[6], Buf("y1", retire([B_sqf]))]
            wd_ring = list(wd_sb) + [ustg[j_][i_].bitcast(BF16) for j_ in range(2) for i_ in range(3)]
            B_wdr = B_wd + [B_u[j_][i_] for j_ in range(2) for i_ in range(3)]
            NWR = len(wd_ring)
            psG = [PS(0, 2), PS(2, 2)]
            psGB = [[PB[0], PB[1]], [PB[2], PB[3]]]
            psU = [PS(4, 1), PS(5, 1)]
            psUB = [PB[4], PB[5]]
            gu_n = [0]
            wd_n = [0]

            def E_load_gu(f):
                sl = gu_n[0] % 3
                gu_n[0] += 1
                P.dma("sync", s_wgu[sl], wgu_sb[sl], wgu_b[f], reads=[B_wgub], writes=[B_wgu[sl]])
                return sl

            def E_load_wd(f):
                sl = wd_n[0] % NWR
                wd_n[0] += 1
                P.dma("sync", s_wd[sl], wd_ring[sl], w_down_b[f * 128:(f + 1) * 128, :], reads=[B_wdb], writes=[B_wdr[sl]])
                return sl

            for tcx in range(4):
                c0 = tcx * 512
                hreads = [B_hT[4 * tcx + i] for i in range(4)] + [B_hTpad]
                if tcx > 0:
                    hreads.append(B_hT[4 * tcx - 1])
                if tcx < 3:
                    hreads.append(B_hT[4 * tcx + 4])
                gu_slots = {}
                gu_slots[0] = E_load_gu(0)
                gu_slots[1] = E_load_gu(1)
                wd_slots = {}
                for f in range(NF):
                    if f + 2 < NF:
                        gu_slots[f + 2] = E_load_gu(f + 2)
                    b = f % 2
                    sl = gu_slots[f]
                    wg = wgu_sb[sl]
                    fns = [MM(psG[b][:, 0:512], wg[:, 0, k, :], hT[:, k, c0 + 1:c0 + 513], start=(k == 0), stop=(k == 7)) for k in range(8)]
                    fns += [MM(psG[b][:, 512:514], wg[:, 0, k, :], hT[:, k, c0 + 513:c0 + 515], start=(k == 0), stop=(k == 7)) for k in range(8)]
                    P.op("tensor", fns, reads=hreads + [B_wgu[sl]], writes=psGB[b])
                    P.op("tensor", [MM(psU[b], wg[:, 1, k, :], hT[:, k, c0 + 2:c0 + 514], start=(k == 0), stop=(k == 7)) for k in range(8)],
                         reads=hreads + [B_wgu[sl]], writes=[psUB[b]])
                    u0, u1, u2 = ustg[b]
                    Bu0, Bu1, Bu2 = B_u[b]
                    P.op("scalar", ACT(u0, psG[b][:, 1:513], AF.Identity, scale=cw[:, f, 1:2], bias=cb[:, f:f + 1]),
                         reads=CONST, writes=[Bu0] + psGB[b])
                    P.op("vector", STT(u1, psG[b][:, 0:512], cw[:, f, 0:1], u0, ALU.mult, ALU.add),
                         reads=[Bu0] + CONST, writes=[Bu1] + psGB[b])
                    P.op("vector", STT(u2, psG[b][:, 2:514], cw[:, f, 2:3], u1, ALU.mult, ALU.add),
                         reads=[Bu1] + CONST, writes=[Bu2] + psGB[b])
                    P.op("scalar", ACT(u0, u2, AF.Silu), reads=[Bu2], writes=[Bu0])
                    P.op("vector", TT(actT[:, f, :], u0, psU[b], ALU.mult), reads=[Bu0], writes=[B_actT, psUB[b]])
                psD8 = PS(0, 8)
                wd_n[0] = 0
                for f in range(min(NWR, NF)):
                    wd_slots[f] = E_load_wd(f)
                for f in range(NF):
                    sl = wd_slots[f]
                    fns = []
                    for tt in range(4):
                        for cc in range(2):
                            fns.append(MM(psD8[:, (tt * 2 + cc) * 512:(tt * 2 + cc + 1) * 512], actT[:, f, tt * 128:(tt + 1) * 128],
                                          wd_ring[sl][:, cc * 512:(cc + 1) * 512], start=(f == 0), stop=(f == NF - 1)))
                    P.op("tensor", fns, reads=[B_actT, B_wdr[sl]], writes=PB)
                    if f + NWR < NF:
                        wd_slots[f + NWR] = E_load_wd(f + NWR)
                for tt in range(4):
                    t = 4 * tcx + tt
                    sl = tt % 2
                    P.op("vector", TT(ystg[sl], psD8[:, tt * 1024:(tt + 1) * 1024], x2res[:, t, :], ALU.add),
                         reads=[B_x2], writes=[B_y[sl], PB[2 * tt], PB[2 * tt + 1]])
                    P.dma("gpsimd", s_y[sl], yout[s, t * 128:(t + 1) * 128, :], ystg[sl], reads=[B_y[sl]])
                    if tcx == 3 and tt >= 2:
                        B_sqf.w = B_sqf.w + retire([B_y[1]])

        for sl in range(2):
            if s_y[sl].count:
                P.wait("gpsimd", (s_y[sl], s_y[sl].count))
        if dbg:
            touch = nc.dram_tensor("touch", [1, 128], F32, kind="ExternalOutput").ap()
            s_t = P.new_sem()
            for i_, ap_ in enumerate([xin[0, 0:1, 0:4], w_in[0:1, 0:4], w_out[0:1, 0:4], w_gate[0:1, 0:4], w_up[0:1, 0:4],
                                      w_down[0:1, 0:4], bandd_d[0:1, 0:4], bandw_d[0:1, 0:4]]):
                P.dma("gpsimd", s_t, touch[:, i_ * 4:(i_ + 1) * 4], ap_)
            P.dma("gpsimd", s_t, yout[0, 0:1, 0:4], xin[0, 0:1, 0:4])
            P.wait("gpsimd", (s_t, s_t.count))
            for k_ in ("scalar", "vector", "gpsimd", "tensor"):
                if P.esem[k_].count:
                    P.wait("gpsimd", (P.esem[k_], P.esem[k_].count))
            for sm_ in [s_init, s_wout, s_band, s_bandw] + s_win4 + s_x + s_wgu + s_wd + list(s_cv.values()):
                if sm_.count:
                    P.wait("gpsimd", (sm_, sm_.count))
            s_dbg = P.new_sem()
            for q in range(4):
                P.dma("gpsimd", s_dbg, dbg_d[:, q * 13120:(q + 1) * 13120], arena[:, q * 13120:(q + 1) * 13120])
            P.wait("gpsimd", (s_dbg, s_dbg.count))
        if info is not None:
            info.update(dict(o_X=o_X, o_Z=o_Z, o_Z3=o_Z3, o_hT=o_hT, o_stat=o_stat, o_sm=o_sm, HTW=HTW, o_xs=o_xs))
        P.run(block)
    return nc


def _rel_bucket(rel):
    nb = 16
    max_exact = 8
    ret = np.where(rel > 0, nb, 0)
    n = np.abs(rel)
    nf = np.maximum(n, 1).astype(np.float32)
    large = max_exact + (np.log(nf / max_exact) / math.log(128 / max_exact) * (nb - max_exact)).astype(np.int32)
    large = np.minimum(large, nb - 1)
    return ret + np.where(n < max_exact, n, large)


_NC_CACHE = {}


def kernel(x_prompt, x_sample, norm_attn_g, w_in, diff_q_norm_g, diff_k_norm_g,
           diff_lambda_q1, diff_lambda_k1, diff_lambda_q2, diff_lambda_k2,
           diff_subln_g, win_q_norm_g, win_k_norm_g, win_sink, rel_bias,
           w_out, norm_ffn_g, w_gate, w_up, conv_w, conv_b, w_down):
    f32 = np.float32
    A = lambda a: np.ascontiguousarray(np.asarray(a, dtype=f32))
    xall = np.concatenate([A(x_prompt), A(x_sample)], axis=0)
    rel_bias = A(rel_bias)
    kl = np.arange(128)[:, None]
    jj = np.arange(384)[None, :]
    idx_band = _rel_bucket(128 + kl - jj)
    bandd = rel_bias[idx_band][:, :, 0:4].transpose(0, 2, 1)
    ql = np.arange(128)[None, None, :]
    kk = np.arange(3)[None, :, None]
    relw = (kk - 1) * 128 + np.arange(128)[:, None, None] - ql
    idx_w = _rel_bucket(relw)
    bw = rel_bias[idx_w][..., 4:12]
    bandw = bw.reshape(128, 3, 128, 2, 2, 2).transpose(0, 3, 5, 1, 4, 2)
    maskw = np.where(np.abs(relw) <= 128, 0.0, NEG).astype(f32)
    common = {
        "w_in": A(w_in)[0], "w_out": A(w_out)[0], "w_gate": A(w_gate)[0], "w_up": A(w_up)[0], "w_down": A(w_down)[0],
        "gA": A(norm_attn_g)[0][None, :], "gF": A(norm_ffn_g)[0][None, :],
        "gcols": np.stack([np.tile(A(diff_q_norm_g)[0], 2), np.tile(A(diff_k_norm_g)[0], 2),
                           np.tile(A(win_q_norm_g)[0], 2), np.tile(A(win_k_norm_g)[0], 2)], axis=1),
        "lamv": np.concatenate([A(diff_lambda_q1)[0], A(diff_lambda_k1)[0], A(diff_lambda_q2)[0], A(diff_lambda_k2)[0]])[None, :],
        "gsub": A(diff_subln_g)[0][None, :],
        "sink": A(win_sink)[0][None, :],
        "cfar": np.concatenate([rel_bias[15, 0:4], rel_bias[31, 0:4]])[None, :],
        "bandd": bandd.reshape(128, 4 * 384),
        "bandw": bandw.reshape(128, 2 * 1536),
        "maskw": maskw.reshape(128, 384),
        "cw": A(conv_w)[0].T.reshape(NF, 128, 3).transpose(1, 0, 2).reshape(128, NF * 3),
        "cb": A(conv_b)[0].reshape(NF, 128).T,
    }
    common = {k: np.ascontiguousarray(v, dtype=f32) for k, v in common.items()}
    if "nc" not in _NC_CACHE:
        _NC_CACHE["nc"] = build_nc()
    nc = _NC_CACHE["nc"]
    in_maps = []
    for c in range(NCORES):
        m = dict(common)
        m["xin"] = np.ascontiguousarray(xall[c * SPC:(c + 1) * SPC])
        in_maps.append(m)
    res = run_bass_kernel_spmd(nc, in_maps, core_ids=list(range(NCORES)))
    yall = np.concatenate([np.asarray(r["yout"], dtype=f32) for r in res.results], axis=0)
    nb = x_prompt.shape[0]
    return (yall[:nb], yall[nb:])
```

```python
import math
from contextlib import ExitStack

import numpy as np
import concourse.bass as bass
import concourse.mybir as mybir
from concourse.bass_utils import run_bass_kernel_spmd

F32 = mybir.dt.float32
BF16 = mybir.dt.bfloat16
AF = mybir.ActivationFunctionType
ALU = mybir.AluOpType
AX = mybir.AxisListType

NCORES = 8
SPC = 3
S = 2048
D = 1024
NT = S // 128
HD = 64
D_IN = 2304
D_FF = 2816
NF = D_FF // 128
EPS = 1e-6
LAM_INIT = 0.8 - 0.6 * math.exp(-0.3 * 0)
NEG = -30000.0

ENGS = ("sync", "scalar", "vector", "gpsimd", "tensor")


class Sem:
    def __init__(self, h):
        self.h = h
        self.count = 0


class Buf:
    def __init__(self, name, base=()):
        self.name = name
        self.w = list(base)
        self.r = {}


def retire(bufs):
    ev = []
    for b in bufs:
        ev.extend(b.w)
        ev.extend(b.r.values())
    return ev


class Prog:
    def __init__(self, nc, es):
        self.nc = nc
        self.es = es
        self.lists = {k: [] for k in ENGS}
        self.waited = {k: {} for k in ENGS}
        self.esem = {}
        for k in ("scalar", "vector", "gpsimd", "tensor"):
            self.esem[k] = Sem(es.enter_context(nc.semaphore("sem_" + k)))
        self.nsem = 0

    def new_sem(self):
        self.nsem += 1
        return Sem(self.es.enter_context(self.nc.semaphore("dsem%d" % self.nsem)))

    dead = False

    def wait(self, eng, ev):
        if self.dead:
            return
        sem, val = ev
        cur = self.waited[eng].get(id(sem), 0)
        if cur >= val:
            return
        self.waited[eng][id(sem)] = val
        h = sem.h
        self.lists[eng].append(lambda e: e.wait_ge(h, val))

    def _deps(self, eng, reads, writes, extra):
        for b in reads:
            for ev in b.w:
                self.wait(eng, ev)
        for b in writes:
            for ev in b.w:
                self.wait(eng, ev)
            for ev in b.r.values():
                self.wait(eng, ev)
        for ev in extra:
            self.wait(eng, ev)

    def op(self, eng, fns, reads=(), writes=(), extra=()):
        if self.dead:
            return None
        if callable(fns):
            fns = [fns]
        self._deps(eng, reads, writes, extra)
        sem = self.esem[eng]
        sem.count += 1
        h = sem.h
        for f in fns[:-1]:
            self.lists[eng].append(f)
        last = fns[-1]
        self.lists[eng].append(lambda e: last(e).then_inc(h, 1))
        ev = (sem, sem.count)
        for b in reads:
            b.r[eng] = ev
        for b in writes:
            b.w = [ev]
            b.r = {}
        return ev

    def dma(self, eng, sem, out, in_, reads=(), writes=(), extra=(), **kw):
        if self.dead:
            return None
        self._deps(eng, reads, writes, extra)
        sem.count += 16
        h = sem.h
        self.lists[eng].append(lambda e: e.dma_start(out=out, in_=in_, **kw).then_inc(h, 16))
        ev = (sem, sem.count)
        for b in reads:
            b.r["dma%d" % id(sem)] = ev
        for b in writes:
            b.w = [ev]
            b.r = {}
        return ev

    def run(self, block):
        for name in ENGS:
            lst = self.lists[name]

            def body(e, lst=lst):
                for f in lst:
                    f(e)

            getattr(block, name)(body)


def MM(out, lhsT, rhs, start=True, stop=True, skip=False):
    return lambda e: e.matmul(out, lhsT=lhsT, rhs=rhs, start=start, stop=stop, skip_group_check=skip)


def TR(out, in_, ident):
    return lambda e: e.transpose(out=out, in_=in_, identity=ident)


def ACT(out, in_, func, **kw):
    return lambda e: e.activation(out=out, in_=in_, func=func, **kw)


def TS(out, in0, s1, s2, op0, op1=None):
    if op1 is None:
        return lambda e: e.tensor_scalar(out=out, in0=in0, scalar1=s1, scalar2=None, op0=op0)
    return lambda e: e.tensor_scalar(out=out, in0=in0, scalar1=s1, scalar2=s2, op0=op0, op1=op1)


def STT(out, in0, scalar, in1, op0, op1):
    return lambda e: e.scalar_tensor_tensor(out=out, in0=in0, scalar=scalar, in1=in1, op0=op0, op1=op1)


def TT(out, in0, in1, op):
    return lambda e: e.tensor_tensor(out=out, in0=in0, in1=in1, op=op)


def TC(out, in_):
    return lambda e: e.tensor_copy(out=out, in_=in_)


def RED(out, in_, op=ALU.add, axis=AX.X):
    return lambda e: e.tensor_reduce(out=out, in_=in_, axis=axis, op=op)


def RCP(out, in_):
    return lambda e: e.reciprocal(out=out, in_=in_)


def MS(ap, val):
    return lambda e: e.memset(ap, val)


def rs(ap, shape):
    if len(shape) == 2:
        return ap
    names = "abcd"[: len(shape) - 1]
    kw = {names[i]: shape[i + 1] for i in range(len(shape) - 1)}
    return ap.rearrange("p (%s) -> p %s" % (" ".join(names), " ".join(names)), **kw)


def bc(ap, shape):
    return ap.unsqueeze(2).broadcast_to(list(shape))


def build_nc(spc=SPC, stop="E", dbg=False, info=None):
    try:
        return _build_nc(spc, stop, dbg, info)
    except Exception as ex:
        if type(ex).__name__ == '_Cut':
            return _LAST[0]
        raise


_LAST = [None]


def _build_nc(spc=SPC, stop="E", dbg=False, info=None):
    nc = bass.Bass("TRN2", target_bir_lowering=False)
    _LAST[0] = nc
    stop_i = "0ABCDE".index(stop)

    def din(name, shape, dt=F32):
        return nc.dram_tensor(name, list(shape), dt, kind="ExternalInput").ap()

    xin = din("xin", [SPC, S, D])
    w_in = din("w_in", [D, D_IN])
    w_out = din("w_out", [D, D])
    w_gate = din("w_gate", [D, D_FF])
    w_up = din("w_up", [D, D_FF])
    w_down = din("w_down", [D_FF, D])
    gA_d = din("gA", [1, D])
    gF_d = din("gF", [1, D])
    gcols_d = din("gcols", [128, 4])
    lamv_d = din("lamv", [1, 256])
    gsub_d = din("gsub", [1, 128])
    sink_d = din("sink", [1, 8])
    cfar_d = din("cfar", [1, 8])
    bandd_d = din("bandd", [128, 4 * 384])
    bandw_d = din("bandw", [128, 2 * 1536])
    maskw_d = din("maskw", [128, 384])
    cw_d = din("cw", [128, NF * 3])
    cb_d = din("cb", [128, NF])
    yout = nc.dram_tensor("yout", [SPC, S, D], F32, kind="ExternalOutput").ap()
    dbg_d = nc.dram_tensor("dbg", [128, 52480], F32, kind="ExternalOutput").ap() if dbg else None

    w_in_b = nc.dram_tensor("w_in_b", [D, D_IN], BF16, kind="Internal").ap()
    w_out_b = nc.dram_tensor("w_out_b", [D, D], BF16, kind="Internal").ap()
    wgu_b = nc.dram_tensor("wgu_b", [NF, 128, 2, 8, 128], BF16, kind="Internal").ap()
    w_down_b = nc.dram_tensor("w_down_b", [D_FF, D], BF16, kind="Internal").ap()

    with ExitStack() as es:
        AW = 52480
        arena = es.enter_context(nc.sbuf_tensor("arena", [128, AW], F32))
        psum = es.enter_context(nc.psum_tensor("psum", [128, 4096], F32))
        P = Prog(nc, es)
        block = es.enter_context(nc.Block())
        import os as _os
        CUT = int(_os.environ.get("K_CUT", "0"))

        class _Cut(Exception):
            pass

        def cut(n):
            if CUT == n:
                s_c = P.new_sem()
                ev_ = P.dma("gpsimd", s_c, yout[0, 0:1, 0:4], xin[0, 0:1, 0:4])
                P.wait("gpsimd", ev_)
                for k_ in ("scalar", "vector", "gpsimd", "tensor"):
                    if P.esem[k_].count:
                        P.wait("gpsimd", (P.esem[k_], P.esem[k_].count))
                P.dead = True

        def V(off, nwords, shape, dt=F32):
            a = arena[:, off:off + nwords]
            if dt != F32:
                a = a.bitcast(dt)
            return rs(a, shape)

        def PS(bank0, nbanks, shape=None, dt=F32):
            a = psum[:, bank0 * 512:(bank0 + nbanks) * 512]
            if dt != F32:
                a = a.bitcast(dt)
            return a if shape is None else rs(a, shape)

        PB = [Buf("bank%d" % i) for i in range(8)]

        cur = [0]

        def alloc(n):
            o = cur[0]
            cur[0] += n
            return o

        o_ident = alloc(64)
        o_gA = alloc(1024)
        o_gF = alloc(1024)
        o_gsub = alloc(128)
        o_gcols = alloc(4)
        o_gcs = alloc(4)
        o_lamv = alloc(256)
        o_lamt = alloc(128)
        o_sm = alloc(64)
        o_mhalf = alloc(32)
        o_eps = alloc(2)
        o_zero = alloc(64)
        o_esink = alloc(8)
        o_cfar = alloc(8)
        o_cw = alloc(NF * 3)
        o_cb = alloc(NF)
        o_mask = alloc(384)
        o_stat = alloc(512)
        o_xs = [alloc(1024), alloc(1024)]
        o_hb = [alloc(512), alloc(512)]
        o_junk = alloc(512)
        o_sqf = alloc(1664)
        HTW = 2052
        o_hT = alloc(8 * HTW // 2)
        o_X = cur[0]
        XW = 19840
        cur[0] += XW
        o_Z = cur[0]
        ZW = 10752
        cur[0] += ZW
        o_Z3 = cur[0]
        Z3W = 4608
        cur[0] += Z3W
        assert cur[0] <= AW, cur[0]

        ident = V(o_ident, 64, [128, 128], BF16)
        gA_b = V(o_gA, 1024, [128, 1024])
        gF_b = V(o_gF, 1024, [128, 1024])
        gsub_b = V(o_gsub, 128, [128, 128])
        gcols = V(o_gcols, 4, [128, 4])
        gcs = V(o_gcs, 4, [128, 4])
        lamv = V(o_lamv, 256, [128, 256])
        lamt = V(o_lamt, 128, [128, 128])
        sm = V(o_sm, 64, [128, 64])
        mhalf = V(o_mhalf, 32, [128, 32])
        epsc = V(o_eps, 2, [128, 2])[:, 0:1]
        zero_b = V(o_zero, 64, [128, 128], BF16)
        esink = V(o_esink, 8, [128, 8])
        cfar = V(o_cfar, 8, [128, 8])
        cw = V(o_cw, NF * 3, [128, NF, 3])
        cb = V(o_cb, NF, [128, NF])
        maskw = V(o_mask, 384, [128, 3, 128])
        stat = V(o_stat, 512, [128, 512])
        xs = [V(o, 1024, [128, 1024]) for o in o_xs]
        hb = [V(o, 512, [128, 1024], BF16) for o in o_hb]
        junk = V(o_junk, 512, [128, 1024], BF16)
        sqf = V(o_sqf, 1664, [128, 1664])
        hT = V(o_hT, 8 * HTW // 2, [128, 8, HTW], BF16)

        oX = o_X
        qT_d = V(oX, 4096, [128, 4, S], BF16); oX += 4096
        kT_d = V(oX, 4096, [128, 4, S], BF16); oX += 4096
        v_d = V(oX, 16 * 4 * 65, [128, NT, 4, 130], BF16); oX += 16 * 4 * 65
        qT_w = V(oX, 4096, [128, 4, S], BF16); oX += 4096
        kT_w = V(oX, 2048, [128, 2, S], BF16); oX += 2048
        v_w = V(oX, 16 * 2 * 33, [128, NT, 2, 66], BF16); oX += 16 * 2 * 33
        assert oX - o_X <= XW, oX - o_X
        x2res = V(o_X, 16384, [128, NT, D])
        aT = [V(o_X + 16384 + i * 512, 512, [128, 8, 128], BF16) for i in range(2)]
        w_in_sb = V(o_Z, 8 * 1152, [128, 8, D_IN], BF16)
        attn_tok = V(o_Z, 8192, [128, NT, D], BF16)
        actT = V(o_Z, NF * 256, [128, NF, 512], BF16)
        oZ = o_Z + NF * 256
        wgu_sb = [V(oZ + i * 1024, 1024, [128, 2, 8, 128], BF16) for i in range(3)]
        oZ += 3 * 1024
        NWD = 4
        wd_sb = [V(oZ + i * 512, 512, [128, 1024], BF16) for i in range(NWD)]
        oZ += NWD * 512
        assert oZ - o_Z <= ZW, oZ - o_Z
        raw = V(o_Z3, 2304, [128, 2304])
        qn2 = [V(o_Z3 + 2304 + i * 896, 896, [128, 1792], BF16) for i in range(2)]
        et = [V(o_Z3 + i * 512, 512, [128, 1024], BF16) for i in range(3)]
        ostg = [V(o_sqf + i * 512, 512, [128, 512]) for i in range(3)]
        adjb = [[V(o_Z3 + 1536 + (ty * 2 + hl) * 768, 768, [128, 4, 384], BF16) for hl in range(2)] for ty in range(2)]
        bandf = V(o_hT, 1536, [128, 4, 384])
        tmpf = V(o_hT + 1536, 384, [128, 384])
        etw = [V(o_hT + 5120 + i * 384, 384, [128, 768], BF16) for i in range(3)]
        bandw = V(o_hT + 2048, 3072, [128, 4, 768])
        w_out_sb = V(o_Z3, 4096, [128, 8, D], BF16)
        ustg = [[V(o_Z3 + (j * 3 + i) * 512, 512, [128, 512]) for i in range(3)] for j in range(2)]
        assert 3072 + 1536 <= Z3W and 1536 + 3072 <= Z3W
        ystg = [V(o_Z3 + 3072, 1024, [128, 1024]), V(o_sqf, 1024, [128, 1024])]

        s_init = P.new_sem()
        s_x = [P.new_sem(), P.new_sem()]
        s_y = [P.new_sem(), P.new_sem()]
        s_win4 = [P.new_sem() for _ in range(4)]
        s_wout = P.new_sem()
        s_band = P.new_sem()
        s_bandw = P.new_sem()
        s_wgu = [P.new_sem() for _ in range(3)]
        s_wd = [P.new_sem() for _ in range(NWD + 6)]
        s_cv = {k: P.new_sem() for k in ("w_in", "w_out", "wgu", "w_down")}

        cut(1)
        B_const = Buf("const")
        init_loads = [
            (gA_b, gA_d.partition_broadcast(128)),
            (gF_b, gF_d.partition_broadcast(128)),
            (gsub_b, gsub_d.partition_broadcast(128)),
            (gcols, gcols_d),
            (lamv, lamv_d.partition_broadcast(128)),
            (esink, sink_d.partition_broadcast(128)),
            (cfar, cfar_d.partition_broadcast(128)),
            (V(o_cw, NF * 3, [128, NF * 3]), cw_d),
            (cb, cb_d),
            (V(o_mask, 384, [128, 384]), maskw_d),
        ]
        import os as _os
        LVL = int(_os.environ.get("K_LVL", "99"))
        for o_, i_ in init_loads[:LVL]:
            P.dma("sync", s_init, o_, i_)
        B_const.w = [(s_init, s_init.count)]

        cut(2)
        B_winb = Buf("w_in_b"); B_woutb = Buf("w_out_b"); B_wgub = Buf("wgu_b"); B_wdb = Buf("w_down_b")

        def conv_w_in():
            P.dma("gpsimd", s_cv["w_in"], w_in_b.rearrange("d (a c) -> (d a) c", a=2),
                  w_in.rearrange("d (a c) -> (d a) c", a=2))
            B_winb.w = [(s_cv["w_in"], s_cv["w_in"].count)]

        def conv_rest():
            P.dma("gpsimd", s_cv["w_out"], w_out_b, w_out)
            B_woutb.w = [(s_cv["w_out"], s_cv["w_out"].count)]
            for m, wsrc in enumerate((w_gate, w_up)):
                for k in range(8):
                    P.dma("gpsimd", s_cv["wgu"],
                          wgu_b[:, :, m, k, :].rearrange("c p f -> p c f"),
                          wsrc[k * 128:(k + 1) * 128, :].rearrange("p (c f) -> p c f", f=128))
            B_wgub.w = [(s_cv["wgu"], s_cv["wgu"].count)]
            P.dma("gpsimd", s_cv["w_down"], w_down_b, w_down)
            B_wdb.w = [(s_cv["w_down"], s_cv["w_down"].count)]

        import os as _os
        if not _os.environ.get('K_SKIP_CONV'):
            conv_w_in()

        cut(3)
        B_ident = Buf("ident"); B_mhalf = Buf("mhalf"); B_lamt = Buf("lamt"); B_sm = Buf("sm")
        idf = V(o_lamt, 128, [128, 128])
        P.op("gpsimd", MS(idf, 0.0), writes=[B_lamt])
        P.op("gpsimd", lambda e: e.affine_select(out=idf, in_=idf, pattern=[[-1, 128]], compare_op=ALU.not_equal,
                                                 fill=1.0, base=0, channel_multiplier=1),
             reads=[B_lamt], writes=[B_lamt])
        P.op("vector", TC(ident, idf), reads=[B_lamt], writes=[B_ident])
        P.op("gpsimd", [MS(mhalf, -0.5), MS(V(o_eps, 2, [128, 2]), EPS)], writes=[B_mhalf])
        B_zero = Buf("zero")
        P.op("gpsimd", MS(zero_b, 0.0), writes=[B_zero])
        B_hT = [Buf("hT%d" % t) for t in range(NT)]
        B_hTpad = Buf("hTpad")

        cut(4)
        P.op("vector", TT(lamt[:, 0:64], lamv[:, 0:64], lamv[:, 64:128], ALU.mult), reads=[B_const, B_ident], writes=[B_lamt])
        P.op("vector", TT(lamt[:, 64:128], lamv[:, 128:192], lamv[:, 192:256], ALU.mult), reads=[B_const], writes=[B_lamt])
        P.op("vector", RED(sm[:, 0:2], rs(lamt, [128, 2, 64])), reads=[B_lamt], writes=[B_sm])
        P.op("scalar", ACT(sm[:, 2:4], sm[:, 0:2], AF.Exp), reads=[B_sm], writes=[B_sm])
        P.op("vector", TS(sm[:, 4:5], sm[:, 3:4], sm[:, 2:3], -LAM_INIT, ALU.subtract, ALU.add), reads=[B_sm], writes=[B_sm])
        negl = sm[:, 4:5]
        B_gc = Buf("gconst")
        P.op("vector", TC(gcs, gcols), reads=[B_const], writes=[B_gc])
        P.op("vector", TS(gcs[:, 0:3:2], gcs[:, 0:3:2], 0.125, None, ALU.mult), reads=[B_gc], writes=[B_gc])
        P.op("vector", TS(gsub_b, gsub_b, 1.0 - LAM_INIT, None, ALU.mult), reads=[B_const], writes=[B_gc])
        P.op("scalar", ACT(esink, esink, AF.Exp), reads=[B_const], writes=[B_gc])
        CONST = [B_const, B_gc, B_sm]

        cut(5)
        class Region:
            def __init__(self):
                self.bufs = []
                self.base = []

            def switch(self, names):
                self.base = retire(self.bufs) + list(self.base)
                best = {}
                for s_, v_ in self.base:
                    if id(s_) not in best or best[id(s_)][1] < v_:
                        best[id(s_)] = (s_, v_)
                self.base = list(best.values())
                self.bufs = [Buf(n, self.base) for n in names]
                return self.bufs

        RX, RZ, RZ3 = Region(), Region(), Region()
        B_xs = [Buf("xs0"), Buf("xs1")]
        B_hb = [Buf("hb0"), Buf("hb1")]
        B_junk = Buf("junk"); B_sqf = Buf("sqf")
        B_stat = [Buf("stat%d" % i) for i in range(8)]

        def rsqrt_pool(out, in_, n, inv_n, tmp, reads, writes):
            P.op("gpsimd", TS(tmp, in_, inv_n, EPS, ALU.mult, ALU.add), reads=reads, writes=writes)
            P.op("gpsimd", TT(out, tmp, mhalf[:, 0:n], ALU.pow), reads=writes + [B_mhalf], writes=writes)

        def rsqrt_act(out, in_, inv_n, tmp, reads, writes):
            P.op("scalar", ACT(tmp, in_, AF.Sqrt, scale=inv_n, bias=EPS), reads=reads, writes=writes)
            P.op("vector", RCP(out, tmp), reads=writes, writes=writes)

        def stcol(slot, j, n=1):
            return stat[:, slot * 64 + j: slot * 64 + j + n]

        for s in range(spc):
            if stop_i < 1:
                continue
            B_qTd, B_kTd, B_vd, B_qTw, B_kTw, B_vw = RX.switch(["qTd", "kTd", "vd", "qTw", "kTw", "vw"])
            B_win4 = RZ.switch(["w_in0", "w_in1", "w_in2", "w_in3"])
            B_raw, B_qn0, B_qn1 = RZ3.switch(["raw", "qn0", "qn1"])
            B_qn2 = [B_qn0, B_qn1]
            P.op("gpsimd", MS(v_d[:, :, :, 128:130], 1.0), writes=[B_vd])
            P.op("gpsimd", MS(v_w[:, :, :, 64:66], 1.0), writes=[B_vw])

            WIN_DEPS = [[B_win4[0]], [B_win4[0]], [B_win4[1]], [B_win4[1], B_win4[2]], [B_win4[2], B_win4[3]]]

            def A_S1a(t):
                sl = t % 2
                P.dma("sync", s_x[sl], xs[sl], xin[s, t * 128:(t + 1) * 128, :], writes=[B_xs[sl]])
                st_ = B_stat[sl]
                P.op("scalar", ACT(junk, xs[sl], AF.Square, accum_out=stcol(sl, 0)), reads=[B_xs[sl]], writes=[B_junk, st_])
                rsqrt_act(stcol(sl, 1), stcol(sl, 0), 1.0 / D, stcol(sl, 2), [st_], [st_])
                P.op("vector", STT(hb[sl], xs[sl], stcol(sl, 1), gA_b, ALU.mult, ALU.mult),
                     reads=[B_xs[sl], st_] + CONST, writes=[B_hb[sl]])

            def A_S1b(t):
                sl = t % 2
                tc0 = 2 + t * 128
                pT = PS(0, 1, None, BF16)
                P.op("tensor", [TR(pT[:, c * 128:(c + 1) * 128], hb[sl][:, c * 128:(c + 1) * 128], ident) for c in range(8)],
                     reads=[B_hb[sl], B_ident], writes=[PB[0]])
                P.op("scalar", ACT(hT[:, :, tc0:tc0 + 128], rs(pT, [128, 8, 128]), AF.Copy), writes=[B_hT[t], PB[0]])

            def A_S2(t):
                tc0 = 2 + t * 128
                psA = PS(1, 5)
                for cc in range(5):
                    n = 512 if cc < 4 else 256
                    P.op("tensor", [MM(psA[:, cc * 512:cc * 512 + n], hT[:, k, tc0:tc0 + 128],
                                       w_in_sb[:, k, cc * 512:cc * 512 + n], start=(k == 0), stop=(k == 7)) for k in range(8)],
                         reads=[B_hT[t]] + WIN_DEPS[cc], writes=[PB[1 + cc]])

            def A_S3a(t):
                psA = PS(1, 5)
                P.op("scalar", ACT(raw[:, 0:1536], psA[:, 0:1536], AF.Copy), writes=[B_raw, PB[1], PB[2], PB[3]])
                P.op("vector", TC(raw[:, 1536:1664], psA[:, 1536:1664]), writes=[B_raw, PB[4]])
                P.op("vector", TC(v_d[:, t, :, 0:128], rs(psA[:, 1664:2176], [128, 4, 128])), writes=[B_vd, PB[4], PB[5]])
                P.op("vector", TC(v_w[:, t, :, 0:64], rs(psA[:, 2176:2304], [128, 2, 64])), writes=[B_vw, PB[5]])

            def A_S3b(t):
                sl = 2 + (t % 2)
                st_ = B_stat[sl]
                qs = t % 2
                qn_ = qn2[qs]
                P.op("scalar", ACT(sqf, raw[:, 0:1664], AF.Square), reads=[B_raw], writes=[B_sqf])
                P.op("vector", RED(stcol(sl, 0, 26), rs(sqf, [128, 26, 64])), reads=[B_sqf], writes=[st_])
                rsqrt_act(stcol(sl, 32, 26), stcol(sl, 0, 26), 1.0 / HD, stcol(sl, 0, 26), [st_], [st_])
                rq = stcol(sl, 32, 26)
                P.op("vector", TT(rs(qn_[:, 0:1536], [128, 24, 64]), rs(raw[:, 0:1536], [128, 24, 64]),
                                  bc(rq[:, 0:24], [128, 24, 64]), ALU.mult), reads=[B_raw, st_], writes=[B_qn2[qs]])
                qk = rs(qn_[:, 1536:1792], [128, 2, 128])
                for dup in range(2):
                    P.op("vector", TT(qk[:, :, dup * 64:(dup + 1) * 64], rs(raw[:, 1536:1664], [128, 2, 64]),
                                      bc(rq[:, 24:26], [128, 2, 64]), ALU.mult), reads=[B_raw, st_], writes=[B_qn2[qs]])

            def A_S4(t):
                pQ = PS(6, 2, None, BF16)
                tcs = slice(t * 128, (t + 1) * 128)
                qs = t % 2
                qn_ = qn2[qs]
                P.op("tensor", [TR(pQ[:, j * 128:(j + 1) * 128], qn_[:, j * 128:(j + 1) * 128], ident) for j in range(14)],
                     reads=[B_qn2[qs], B_ident], writes=[PB[6], PB[7]])
                P.op("scalar", ACT(qT_d[:, :, tcs], rs(pQ[:, 0:512], [128, 4, 128]), AF.Identity, scale=gcs[:, 0:1]),
                     reads=CONST, writes=[B_qTd, PB[6]])
                P.op("scalar", ACT(kT_d[:, :, tcs], rs(pQ[:, 512:1024], [128, 4, 128]), AF.Identity, scale=gcs[:, 1:2]),
                     reads=CONST, writes=[B_kTd, PB[6]])
                P.op("scalar", ACT(qT_w[:, :, tcs], rs(pQ[:, 1024:1536], [128, 4, 128]), AF.Identity, scale=gcs[:, 2:3]),
                     reads=CONST, writes=[B_qTw, PB[7]])
                P.op("vector", TS(kT_w[:, :, tcs], rs(pQ[:, 1536:1792], [128, 2, 128]), gcs[:, 3:4], None, ALU.mult),
                     reads=CONST, writes=[B_kTw, PB[7]])

            def load_w_in():
                wv_ = w_in_b.rearrange("(k p) c -> p k c", p=128)
                for gi, (d0, d1, s0) in enumerate(((0, 1024, 0), (1536, 2176, 1024), (1024, 1536, 1664), (2176, 2304, 2176))):
                    P.dma("gpsimd" if s > 0 else "sync", s_win4[gi], w_in_sb[:, :, s0:s0 + (d1 - d0)], wv_[:, :, d0:d1],
                          reads=[B_winb], writes=[B_win4[gi]])

            if s > 0:
                load_w_in()
            A_S1a(0)
            A_S1a(1)
            A_S1b(0)
            A_S1b(1)
            A_S1a(2)
            NPRE = NT if s == 0 else 3
            for t in range(2, NPRE):
                A_S1b(t)
                if t + 1 < NT:
                    A_S1a(t + 1)
            if s == 0:
                load_w_in()
            for t in range(NT):
                A_S2(t)
                A_S3a(t)
                if NPRE <= t + 2 < NT:
                    A_S1b(t + 2)
                if NPRE <= t + 2 and t + 3 < NT:
                    A_S1a(t + 3)
                if t >= 1:
                    A_S4(t - 1)
                A_S3b(t)
            A_S4(NT - 1)
            if s == 0:
                P.wait("gpsimd", (P.esem["tensor"], P.esem["tensor"].count))
                conv_rest()

            if stop_i < 2:
                continue
            (B_attn,) = RZ.switch(["attn_tok"])
            zb3 = RZ3.switch(["et0", "et1", "et2", "adj0", "adj1", "adj2", "adj3"])
            B_et = zb3[0:3]
            B_adj = zb3[3:7]
            base_sq = retire([B_sqf])
            B_o0, B_o1, B_o2 = Buf("o0", base_sq), Buf("o1", base_sq), Buf("o2", base_sq)
            base_h = retire(B_hT + [B_hTpad])
            B_bandf, B_tmpf = Buf("bandf", base_h), Buf("tmpf", base_h)
            P.dma("sync", s_band, V(o_hT, 1536, [128, 1536]), bandd_d, writes=[B_bandf])
            B_bandw = Buf("bandw", base_h)
            B_etw = [Buf("etw%d" % i_, base_h) for i_ in range(3)]
            P.dma("sync", s_bandw, V(o_hT + 2048, 3072, [128, 3072]), bandw_d, writes=[B_bandw])
            for h_ in range(4):
                for ty in range(2):
                    P.op("vector", TS(tmpf, bandf[:, h_, :], cfar[:, ty * 4 + h_:ty * 4 + h_ + 1], None, ALU.subtract),
                         reads=[B_bandf] + CONST, writes=[B_tmpf])
                    P.op("vector", TC(adjb[ty][0][:, h_, :], tmpf), reads=[B_tmpf], writes=[B_adj[h_]])
                    P.op("vector", TT(adjb[ty][1][:, h_, :], tmpf, adjb[ty][0][:, h_, :], ALU.subtract),
                         reads=[B_tmpf, B_adj[h_]], writes=[B_adj[h_]])
            stB = [PS(0, 2), PS(2, 2)]
            stBB = [[PB[0], PB[1]], [PB[2], PB[3]]]
            acc = PS(4, 4)
            accB = [PB[4], PB[5], PB[6], PB[7]]
            tiles = [(h, qc, kb) for h in range(4) for qc in range(4) for kb in range(16)]

            def B_QK(n):
                h, qc, kb = tiles[n]
                b = n % 2
                kc = slice(kb * 128, (kb + 1) * 128)
                qcs = slice(qc * 512, (qc + 1) * 512)
                fns = [MM(stB[b][:, 0:512], kT_d[0:64, h, kc], qT_d[0:64, h, qcs]),
                       MM(stB[b][:, 512:1024], kT_d[64:128, h, kc], qT_d[64:128, h, qcs])]
                rd = [B_kTd, B_qTd]
                lo = max(4 * qc, kb - 1)
                hi = min(4 * qc + 3, kb + 1)
                far31 = (4 * qc < kb - 1)
                ty = 1 if far31 else 0
                cf = cfar[:, ty * 4 + h:ty * 4 + h + 1]
                if lo <= hi:
                    ncol = (hi - lo + 1) * 128
                    c0 = (lo - 4 * qc) * 128
                    j0 = (lo - kb + 1) * 128
                    for m in range(2):
                        for hl in range(2):
                            fns.append(MM(stB[b][:, m * 512 + c0:m * 512 + c0 + ncol], ident,
                                          adjb[ty][hl][:, h, j0:j0 + ncol], start=False, stop=(hl == 1), skip=True))
                    rd = rd + [B_adj[h], B_ident]
                P.op("tensor", fns, reads=rd, writes=stBB[b])
                e_ = n % 3
                P.op("scalar", ACT(et[e_], stB[b], AF.Exp, bias=cf), reads=CONST, writes=[B_et[e_]] + stBB[b])

            def B_PV(n):
                h, qc, kb = tiles[n]
                e_ = n % 3
                fns = []
                for j in range(4):
                    for m in range(2):
                        fns.append(MM(acc[:, j * 512 + m * 256: j * 512 + m * 256 + 129],
                                      et[e_][:, m * 512 + j * 128: m * 512 + (j + 1) * 128],
                                      v_d[:, kb, h, 0:129], start=(kb == 0 and m == 0), stop=(kb == 15), skip=True))
                P.op("tensor", fns, reads=[B_et[e_], B_vd], writes=accB)

            def B_EPI1(h, qc):
                st_ = B_stat[4]
                a3 = rs(acc, [128, 4, 512])
                r = rs(stcol(4, 0, 8), [128, 4, 2])
                P.op("vector", RCP(r, rs(acc, [128, 4, 2, 256])[:, :, :, 128]),
                     writes=[st_] + accB)
                P.op("vector", TS(stcol(4, 8, 4), r[:, :, 1], negl, None, ALU.mult), reads=[st_] + CONST, writes=[st_])
                o1 = rs(ostg[0], [128, 4, 128]); o2 = rs(ostg[1], [128, 4, 128])
                P.op("vector", TT(o1, a3[:, :, 0:128], r[:, :, 0:1].broadcast_to([128, 4, 128]), ALU.mult),
                     reads=[st_], writes=[B_o0] + accB)
                P.op("vector", TT(o2, a3[:, :, 256:384], bc(stcol(4, 8, 4), [128, 4, 128]), ALU.mult),
                     reads=[st_], writes=[B_o1] + accB)

            def B_EPI2a(h, qc):
                st_ = B_stat[7]
                o3 = rs(ostg[2], [128, 4, 128])
                P.op("vector", TT(ostg[0], ostg[0], ostg[1], ALU.add), reads=[B_o1], writes=[B_o0])
                P.op("vector", TT(ostg[2], ostg[0], ostg[0], ALU.mult), reads=[B_o0], writes=[B_o2])
                P.op("vector", RED(stcol(7, 12, 4), o3), reads=[B_o2], writes=[st_])
                P.op("scalar", ACT(stcol(7, 20, 4), stcol(7, 12, 4), AF.Ln, scale=1.0 / 128, bias=epsc), reads=[st_, B_mhalf], writes=[st_])
                P.op("scalar", ACT(stcol(7, 16, 4), stcol(7, 20, 4), AF.Exp, scale=-0.5), reads=[st_], writes=[st_])

            def B_EPI2b(h, qc):
                st_ = B_stat[7]
                o1 = rs(ostg[0], [128, 4, 128])
                for j in range(4):
                    qb = 4 * qc + j
                    P.op("vector", STT(attn_tok[:, qb, h * 128:(h + 1) * 128], o1[:, j, :], stcol(7, 16 + j), gsub_b,
                                       ALU.mult, ALU.mult), reads=[B_o0, st_] + CONST, writes=[B_attn])

            NB = len(tiles)
            B_QK(0)
            B_QK(1)
            NWARM = int(_os.environ.get("K_WARM", "5"))
            pend = None
            for n in range(NB):
                B_PV(n)
                if n + 2 < NB:
                    B_QK(n + 2)
                h, qc, kb = tiles[n]
                if kb == 1 and pend is not None:
                    B_EPI2a(*pend)
                if kb == 5 and pend is not None:
                    B_EPI2b(*pend)
                    pend = None
                if kb == 15:
                    B_EPI1(h, qc)
                    pend = (h, qc)
                elif NWARM and not (max(4 * qc, kb - 1) <= min(4 * qc + 3, kb + 1)):
                    P.wait("tensor", B_zero.w[0])
                    for _w in range(NWARM):
                        P.lists["tensor"].append(MM(acc[:, 386:512], zero_b, v_d[:, 0, 0, 0:126], start=False, stop=False, skip=True))

            if stop_i < 3:
                continue
            if pend is not None:
                B_EPI2a(*pend)
                B_EPI2b(*pend)
            B_sqf.w = B_sqf.w + retire([B_o0, B_o1, B_o2])
            (B_wout,) = RZ3.switch(["w_out_sb"])
            P.dma("sync", s_wout, w_out_sb, w_out_b.rearrange("(k p) n -> p k n", p=128), reads=[B_woutb], writes=[B_wout])
            for gp in range(4):
                P.op("vector", TT(rs(bandw[:, gp, :], [128, 3, 2, 128]), rs(bandw[:, gp, :], [128, 3, 2, 128]),
                                  maskw.unsqueeze(2).broadcast_to([128, 3, 2, 128]), ALU.add),
                     reads=[B_bandw, B_const], writes=[B_bandw])
            stW = [PS(0, 2), PS(2, 2), PS(4, 2)]
            stWB = [[PB[0], PB[1]], [PB[2], PB[3]], [PB[4], PB[5]]]
            accW = [PS(6, 1), PS(7, 1)]
            wt = [(g, qb, ph) for g in range(2) for qb in range(NT) for ph in range(2)]

            def C_kks(qb):
                return [kk for kk in range(3) if 0 <= qb + kk - 1 < NT]

            def C_QK(m):
                g, qb, ph = wt[m]
                b = m % 3
                kks = C_kks(qb)
                fns = []
                for kk in kks:
                    kb = qb + kk - 1
                    for j in range(2):
                        hq = 4 * g + 2 * j + ph
                        fns.append(MM(stW[b][:, (kk * 2 + j) * 128:(kk * 2 + j + 1) * 128],
                                      kT_w[ph * 64:(ph + 1) * 64, g, kb * 128:(kb + 1) * 128],
                                      qT_w[ph * 64:(ph + 1) * 64, hq // 2, qb * 128:(qb + 1) * 128]))
                P.op("tensor", fns, reads=[B_kTw, B_qTw], writes=stWB[b])
                c0, c1 = kks[0] * 256, (kks[-1] + 1) * 256
                P.op("vector", TT(stW[b][:, c0:c1], stW[b][:, c0:c1], bandw[:, g * 2 + ph, c0:c1], ALU.add),
                     reads=[B_bandw], writes=stWB[b])
                P.op("scalar", ACT(etw[b][:, c0:c1], stW[b][:, c0:c1], AF.Exp), writes=[B_etw[b]] + stWB[b])

            def C_PV(m):
                g, qb, ph = wt[m]
                b = m % 3
                a = (g * NT + qb) % 2
                kks = C_kks(qb)
                fns = []
                first = (ph == 0)
                for j in range(2):
                    i = 2 * j + ph
                    for kk in kks:
                        kb = qb + kk - 1
                        fns.append(MM(accW[a][:, i * 128:i * 128 + 65],
                                      etw[b][:, (kk * 2 + j) * 128:(kk * 2 + j + 1) * 128],
                                      v_w[:, kb, g, 0:65], start=first, stop=(kk == kks[-1]), skip=True))
                        first = False
                P.op("tensor", fns, reads=[B_etw[b], B_vw], writes=[PB[6 + a]])
                if ph == 0:
                    return
                st_ = B_stat[5 + a]
                sl = 5 + a
                a3 = rs(accW[a], [128, 4, 128])
                P.op("vector", TT(stcol(sl, 0, 4), a3[:, :, 64], esink[:, 4 * g:4 * g + 4], ALU.add),
                     reads=CONST, writes=[st_, PB[6 + a]])
                P.op("vector", RCP(stcol(sl, 4, 4), stcol(sl, 0, 4)), reads=[st_], writes=[st_])
                P.op("vector", TT(rs(attn_tok[:, qb, 512 + g * 256:512 + (g + 1) * 256], [128, 4, 64]), a3[:, :, 0:64],
                                  bc(stcol(sl, 4, 4), [128, 4, 64]), ALU.mult),
                     reads=[st_], writes=[B_attn, PB[6 + a]])

            NW = len(wt)
            C_QK(0)
            C_QK(1)
            for m in range(NW):
                if m + 2 < NW:
                    C_QK(m + 2)
                C_PV(m)

            ev_h = retire([B_bandf, B_tmpf, B_bandw] + B_etw)
            for b_ in B_hT + [B_hTpad]:
                b_.w = b_.w + ev_h
            P.op("gpsimd", [MS(hT[:, :, 0:2], 0.0), MS(hT[:, :, 2050:2052], 0.0)], writes=[B_hTpad])
            if stop_i < 4:
                continue
            B_x2, B_aT0, B_aT1 = RX.switch(["x2res", "aT0", "aT1"])
            B_aT = [B_aT0, B_aT1]
            psD = [PS(1, 2), PS(4, 2)]
            psDB = [[PB[1], PB[2]], [PB[4], PB[5]]]

            def D_S1(t):
                sl = t % 2
                pT = PS(0, 1, None, BF16)
                P.op("tensor", [TR(pT[:, c * 128:(c + 1) * 128], attn_tok[:, t, c * 128:(c + 1) * 128], ident) for c in range(8)],
                     reads=[B_attn, B_ident], writes=[PB[0]])
                P.op("vector", TC(aT[sl], rs(pT, [128, 8, 128])), writes=[B_aT[sl], PB[0]])
                P.dma("sync", s_x[sl], xs[sl], xin[s, t * 128:(t + 1) * 128, :], writes=[B_xs[sl]])

            def D_S2(t):
                sl = t % 2
                for cc in range(2):
                    P.op("tensor", [MM(psD[sl][:, cc * 512:(cc + 1) * 512], aT[sl][:, c, :],
                                       w_out_sb[:, c, cc * 512:(cc + 1) * 512], start=(c == 0), stop=(c == 7)) for c in range(8)],
                         reads=[B_aT[sl], B_wout], writes=[psDB[sl][cc]])

            def D_S3(t):
                sl = t % 2
                st_ = B_stat[sl]
                P.op("vector", TT(x2res[:, t, :], psD[sl], xs[sl], ALU.add), reads=[B_xs[sl]], writes=[B_x2] + psDB[sl])
                P.op("scalar", ACT(junk, x2res[:, t, :], AF.Square, accum_out=stcol(sl, 0)), reads=[B_x2], writes=[B_junk, st_])
                rsqrt_act(stcol(sl, 1), stcol(sl, 0), 1.0 / D, stcol(sl, 2), [st_], [st_])
                P.op("vector", STT(hb[sl], x2res[:, t, :], stcol(sl, 1), gF_b, ALU.mult, ALU.mult),
                     reads=[B_x2, st_] + CONST, writes=[B_hb[sl]])

            def D_S4(t):
                sl = t % 2
                tc0 = 2 + t * 128
                pT2 = PS(3, 1, None, BF16)
                P.op("tensor", [TR(pT2[:, c * 128:(c + 1) * 128], hb[sl][:, c * 128:(c + 1) * 128], ident) for c in range(8)],
                     reads=[B_hb[sl], B_ident], writes=[PB[3]])
                P.op("scalar", ACT(hT[:, :, tc0:tc0 + 128], rs(pT2, [128, 8, 128]), AF.Copy), writes=[B_hT[t], PB[3]])

            D_S1(0)
            for t in range(NT):
                if t + 1 < NT:
                    D_S1(t + 1)
                D_S2(t)
                D_S3(t)
                if t >= 1:
                    D_S4(t - 1)
            D_S4(NT - 1)

            if stop_i < 5:
                continue
            zb = RZ.switch(["actT"] + ["wgu%d" % i for i in range(3)] + ["wd%d" % i for i in range(NWD)])
            B_actT = zb[0]; B_wgu = zb[1:4]; B_wd = list(zb[4:4 + NWD])
            ub = RZ3.switch(["u%d%d" % (j, i) for j in range(2) for i in range(3)] + ["y0"])
            B_u = [ub[0:3], ub[3:6]]
            B_y = [ub[6], Buf("y1", retire([B_sqf]))]
            wd_ring = list(wd_sb) + [ustg[j_][i_].bitcast(BF16) for j_ in range(2) for i_ in range(3)]
            B_wdr = B_wd + [B_u[j_][i_] for j_ in range(2) for i_ in range(3)]
            NWR = len(wd_ring)
            psG = [PS(0, 2), PS(2, 2)]
            psGB = [[PB[0], PB[1]], [PB[2], PB[3]]]
            psU = [PS(4, 1), PS(5, 1)]
            psUB = [PB[4], PB[5]]
            gu_n = [0]
            wd_n = [0]

            def E_load_gu(f):
                sl = gu_n[0] % 3
                gu_n[0] += 1
                P.dma("sync", s_wgu[sl], wgu_sb[sl], wgu_b[f], reads=[B_wgub], writes=[B_wgu[sl]])
                return sl

            def E_load_wd(f):
                sl = wd_n[0] % NWR
                wd_n[0] += 1
                P.dma("sync", s_wd[sl], wd_ring[sl], w_down_b[f * 128:(f + 1) * 128, :], reads=[B_wdb], writes=[B_wdr[sl]])
                return sl

            for tcx in range(4):
                c0 = tcx * 512
                hreads = [B_hT[4 * tcx + i] for i in range(4)] + [B_hTpad]
                if tcx > 0:
                    hreads.append(B_hT[4 * tcx - 1])
                if tcx < 3:
                    hreads.append(B_hT[4 * tcx + 4])
                gu_slots = {}
                gu_slots[0] = E_load_gu(0)
                gu_slots[1] = E_load_gu(1)
                wd_slots = {}
                for f in range(NF):
                    if f + 2 < NF:
                        gu_slots[f + 2] = E_load_gu(f + 2)
                    b = f % 2
                    sl = gu_slots[f]
                    wg = wgu_sb[sl]
                    fns = [MM(psG[b][:, 0:512], wg[:, 0, k, :], hT[:, k, c0 + 1:c0 + 513], start=(k == 0), stop=(k == 7)) for k in range(8)]
                    fns += [MM(psG[b][:, 512:514], wg[:, 0, k, :], hT[:, k, c0 + 513:c0 + 515], start=(k == 0), stop=(k == 7)) for k in range(8)]
                    P.op("tensor", fns, reads=hreads + [B_wgu[sl]], writes=psGB[b])
                    P.op("tensor", [MM(psU[b], wg[:, 1, k, :], hT[:, k, c0 + 2:c0 + 514], start=(k == 0), stop=(k == 7)) for k in range(8)],
                         reads=hreads + [B_wgu[sl]], writes=[psUB[b]])
                    u0, u1, u2 = ustg[b]
                    Bu0, Bu1, Bu2 = B_u[b]
                    P.op("scalar", ACT(u0, psG[b][:, 1:513], AF.Identity, scale=cw[:, f, 1:2], bias=cb[:, f:f + 1]),
                         reads=CONST, writes=[Bu0] + psGB[b])
                    P.op("vector", STT(u1, psG[b][:, 0:512], cw[:, f, 0:1], u0, ALU.mult, ALU.add),
                         reads=[Bu0] + CONST, writes=[Bu1] + psGB[b])
                    P.op("vector", STT(u2, psG[b][:, 2:514], cw[:, f, 2:3], u1, ALU.mult, ALU.add),
                         reads=[Bu1] + CONST, writes=[Bu2] + psGB[b])
                    P.op("scalar", ACT(u0, u2, AF.Silu), reads=[Bu2], writes=[Bu0])
                    P.op("vector", TT(actT[:, f, :], u0, psU[b], ALU.mult), reads=[Bu0], writes=[B_actT, psUB[b]])
                psD8 = PS(0, 8)
                wd_n[0] = 0
                for f in range(min(NWR, NF)):
                    wd_slots[f] = E_load_wd(f)
                for f in range(NF):
                    sl = wd_slots[f]
                    fns = []
                    for tt in range(4):
                        for cc in range(2):
                            fns.append(MM(psD8[:, (tt * 2 + cc) * 512:(tt * 2 + cc + 1) * 512], actT[:, f, tt * 128:(tt + 1) * 128],
                                          wd_ring[sl][:, cc * 512:(cc + 1) * 512], start=(f == 0), stop=(f == NF - 1)))
                    P.op("tensor", fns, reads=[B_actT, B_wdr[sl]], writes=PB)
                    if f + NWR < NF:
                        wd_slots[f + NWR] = E_load_wd(f + NWR)
                for tt in range(4):
                    t = 4 * tcx + tt
                    sl = tt % 2
                    P.op("vector", TT(ystg[sl], psD8[:, tt * 1024:(tt + 1) * 1024], x2res[:, t, :], ALU.add),
                         reads=[B_x2], writes=[B_y[sl], PB[2 * tt], PB[2 * tt + 1]])
                    P.dma("gpsimd", s_y[sl], yout[s, t * 128:(t + 1) * 128, :], ystg[sl], reads=[B_y[sl]])
                    if tcx == 3 and tt >= 2:
                        B_sqf.w = B_sqf.w + retire([B_y[1]])

        for sl in range(2):
            if s_y[sl].count:
                P.wait("gpsimd", (s_y[sl], s_y[sl].count))
        if dbg:
            touch = nc.dram_tensor("touch", [1, 128], F32, kind="ExternalOutput").ap()
            s_t = P.new_sem()
            for i_, ap_ in enumerate([xin[0, 0:1, 0:4], w_in[0:1, 0:4], w_out[0:1, 0:4], w_gate[0:1, 0:4], w_up[0:1, 0:4],
                                      w_down[0:1, 0:4], bandd_d[0:1, 0:4], bandw_d[0:1, 0:4]]):
                P.dma("gpsimd", s_t, touch[:, i_ * 4:(i_ + 1) * 4], ap_)
            P.dma("gpsimd", s_t, yout[0, 0:1, 0:4], xin[0, 0:1, 0:4])
            P.wait("gpsimd", (s_t, s_t.count))
            for k_ in ("scalar", "vector", "gpsimd", "tensor"):
                if P.esem[k_].count:
                    P.wait("gpsimd", (P.esem[k_], P.esem[k_].count))
            for sm_ in [s_init, s_wout, s_band, s_bandw] + s_win4 + s_x + s_wgu + s_wd + list(s_cv.values()):
                if sm_.count:
                    P.wait("gpsimd", (sm_, sm_.count))
            s_dbg = P.new_sem()
            for q in range(4):
                P.dma("gpsimd", s_dbg, dbg_d[:, q * 13120:(q + 1) * 13120], arena[:, q * 13120:(q + 1) * 13120])
            P.wait("gpsimd", (s_dbg, s_dbg.count))
        if info is not None:
            info.update(dict(o_X=o_X, o_Z=o_Z, o_Z3=o_Z3, o_hT=o_hT, o_stat=o_stat, o_sm=o_sm, HTW=HTW, o_xs=o_xs))
        P.run(block)
    return nc


def _rel_bucket(rel):
    nb = 16
    max_exact = 8
    ret = np.where(rel > 0, nb, 0)
    n = np.abs(rel)
    nf = np.maximum(n, 1).astype(np.float32)
    large = max_exact + (np.log(nf / max_exact) / math.log(128 / max_exact) * (nb - max_exact)).astype(np.int32)
    large = np.minimum(large, nb - 1)
    return ret + np.where(n < max_exact, n, large)


_NC_CACHE = {}


def kernel(x_prompt, x_sample, norm_attn_g, w_in, diff_q_norm_g, diff_k_norm_g,
           diff_lambda_q1, diff_lambda_k1, diff_lambda_q2, diff_lambda_k2,
           diff_subln_g, win_q_norm_g, win_k_norm_g, win_sink, rel_bias,
           w_out, norm_ffn_g, w_gate, w_up, conv_w, conv_b, w_down):
    f32 = np.float32
    A = lambda a: np.ascontiguousarray(np.asarray(a, dtype=f32))
    xall = np.concatenate([A(x_prompt), A(x_sample)], axis=0)
    rel_bias = A(rel_bias)
    kl = np.arange(128)[:, None]
    jj = np.arange(384)[None, :]
    idx_band = _rel_bucket(128 + kl - jj)
    bandd = rel_bias[idx_band][:, :, 0:4].transpose(0, 2, 1)
    ql = np.arange(128)[None, None, :]
    kk = np.arange(3)[None, :, None]
    relw = (kk - 1) * 128 + np.arange(128)[:, None, None] - ql
    idx_w = _rel_bucket(relw)
    bw = rel_bias[idx_w][..., 4:12]
    bandw = bw.reshape(128, 3, 128, 2, 2, 2).transpose(0, 3, 5, 1, 4, 2)
    maskw = np.where(np.abs(relw) <= 128, 0.0, NEG).astype(f32)
    common = {
        "w_in": A(w_in)[0], "w_out": A(w_out)[0], "w_gate": A(w_gate)[0], "w_up": A(w_up)[0], "w_down": A(w_down)[0],
        "gA": A(norm_attn_g)[0][None, :], "gF": A(norm_ffn_g)[0][None, :],
        "gcols": np.stack([np.tile(A(diff_q_norm_g)[0], 2), np.tile(A(diff_k_norm_g)[0], 2),
                           np.tile(A(win_q_norm_g)[0], 2), np.tile(A(win_k_norm_g)[0], 2)], axis=1),
        "lamv": np.concatenate([A(diff_lambda_q1)[0], A(diff_lambda_k1)[0], A(diff_lambda_q2)[0], A(diff_lambda_k2)[0]])[None, :],
        "gsub": A(diff_subln_g)[0][None, :],
        "sink": A(win_sink)[0][None, :],
        "cfar": np.concatenate([rel_bias[15, 0:4], rel_bias[31, 0:4]])[None, :],
        "bandd": bandd.reshape(128, 4 * 384),
        "bandw": bandw.reshape(128, 2 * 1536),
        "maskw": maskw.reshape(128, 384),
        "cw": A(conv_w)[0].T.reshape(NF, 128, 3).transpose(1, 0, 2).reshape(128, NF * 3),
        "cb": A(conv_b)[0].reshape(NF, 128).T,
    }
    common = {k: np.ascontiguousarray(v, dtype=f32) for k, v in common.items()}
    if "nc" not in _NC_CACHE:
        _NC_CACHE["nc"] = build_nc()
    nc = _NC_CACHE["nc"]
    in_maps = []
    for c in range(NCORES):
        m = dict(common)
        m["xin"] = np.ascontiguousarray(xall[c * SPC:(c + 1) * SPC])
        in_maps.append(m)
    res = run_bass_kernel_spmd(nc, in_maps, core_ids=list(range(NCORES)))
    yall = np.concatenate([np.asarray(r["yout"], dtype=f32) for r in res.results], axis=0)
    nb = x_prompt.shape[0]
    return (yall[:nb], yall[nb:])
```
